# Optimizing a Trainium2 kernel written in Bass

```python
import math
import jax
import jax.numpy as jnp
from jax import lax
import numpy as np

D_MODEL = 1024
BATCH = 8
SEQ = 2048
DEPTH = 2

GRID_W = 64
CTX_LEN = 256
N_MIXERS = 2
N_DN_LAYERS = (DEPTH + N_MIXERS - 1) // N_MIXERS
N_DA_LAYERS = DEPTH // N_MIXERS

DN_HEADS = 8
DN_DK = 128
DN_DV = 128
DN_CONV = 5
DN_CHUNK = 64

DA_HEADS = 8
DA_DIM = 64
DA_QBLOCK = 128
ROPE_BASE = 10000.0

FFN_HIDDEN = -(-(8 * D_MODEL) // (3 * 256)) * 256

DEEPNORM_ALPHA = (2.0 * DEPTH) ** 0.25
DEEPNORM_BETA = (8.0 * DEPTH) ** -0.25
LN_EPS = 1e-5
RMS_EPS = 1e-6

kernel_name = 'hybrid_gdn_diffattn_prefix_trunk'


def _layernorm(x, g, b):
    xf = x.astype(jnp.float32)
    mu = jnp.mean(xf, axis=-1, keepdims=True)
    var = jnp.mean(jnp.square(xf - mu), axis=-1, keepdims=True)
    return ((xf - mu) * lax.rsqrt(var + LN_EPS)).astype(x.dtype) * g + b


def _rmsnorm(x, g):
    xf = x.astype(jnp.float32)
    y = xf * lax.rsqrt(jnp.mean(jnp.square(xf), axis=-1, keepdims=True) + RMS_EPS)
    return y.astype(x.dtype) * g


def _l2norm(x):
    xf = x.astype(jnp.float32)
    return xf * lax.rsqrt(jnp.sum(jnp.square(xf), axis=-1, keepdims=True) + RMS_EPS)


def _swiglu(h, w_gate, w_up, w_down):
    return (jax.nn.silu(h @ w_gate) * (h @ w_up)) @ w_down


def _centred_dwconv(x, w):
    pad = (DN_CONV - 1) // 2
    return lax.conv_general_dilated(
        x, w[:, None, :].astype(x.dtype), window_strides=(1,), padding=[(pad, pad)],
        dimension_numbers=('NWC', 'WIO', 'NWC'), feature_group_count=x.shape[-1])


def _axial_rope(rows, dtype):
    quarter = DA_DIM // 4
    inv_freq = ROPE_BASE ** (-jnp.arange(quarter, dtype=jnp.float32) / quarter)
    row = jnp.repeat(jnp.arange(rows, dtype=jnp.float32), GRID_W)
    col = jnp.tile(jnp.arange(GRID_W, dtype=jnp.float32), rows)
    ang_r = row[:, None] * inv_freq
    ang_c = col[:, None] * inv_freq
    ang = jnp.concatenate([ang_r, ang_r, ang_c, ang_c], axis=-1)
    return jnp.cos(ang).astype(dtype), jnp.sin(ang).astype(dtype)


def _apply_rope(x, cos, sin):
    quarter = DA_DIM // 4
    half = DA_DIM // 2

    def rot_half(u):
        return jnp.concatenate([-u[..., quarter:], u[..., :quarter]], axis=-1)

    x_rot = jnp.concatenate([rot_half(x[..., :half]), rot_half(x[..., half:])], axis=-1)
    return x * cos + x_rot * sin


def _flip_seq(t, direction):
    return jnp.flip(t, axis=2) if direction == 1 else t


def _chunked_gated_delta(q, k, v, g, beta, s0):
    bsz, nh, L, _ = q.shape
    dv = v.shape[-1]
    nc = L // DN_CHUNK

    def blocks(t):
        return t.reshape(bsz, nh, nc, DN_CHUNK, *t.shape[3:])

    q, k, v, g, beta = (blocks(t) for t in (q, k, v.astype(jnp.float32), g, beta))
    G = jnp.cumsum(g, axis=-1)
    idx = jnp.arange(DN_CHUNK)
    incl = idx[:, None] >= idx[None, :]
    strict = idx[:, None] > idx[None, :]
    gamma = jnp.exp(jnp.where(incl, G[..., :, None] - G[..., None, :], -jnp.inf))
    kb = k * beta[..., None]
    a_low = jnp.where(strict, jnp.einsum('bhnid,bhnjd->bhnij', kb, k) * gamma, 0.0)
    eye = jnp.eye(DN_CHUNK, dtype=jnp.float32)
    rhs = jnp.concatenate([v * beta[..., None], kb * jnp.exp(G)[..., None]], axis=-1)
    uw = lax.linalg.triangular_solve(eye + a_low, rhs, left_side=True, lower=True, unit_diagonal=True)
    u, w = uw[..., :dv], uw[..., dv:]
    qk = jnp.einsum('bhnid,bhnjd->bhnij', q, k) * gamma
    g_last = G[..., -1]
    q_dec = q * jnp.exp(G)[..., None]
    k_dec = k * jnp.exp(g_last[..., None] - G)[..., None]

    def step(s, inp):
        u_n, w_n, qk_n, qd_n, kd_n, gl_n = inp
        v_new = u_n - jnp.einsum('bhck,bhkv->bhcv', w_n, s)
        o_n = jnp.einsum('bhck,bhkv->bhcv', qd_n, s) + jnp.einsum('bhij,bhjv->bhiv', qk_n, v_new)
        s = s * jnp.exp(gl_n)[..., None, None] + jnp.einsum('bhck,bhcv->bhkv', kd_n, v_new)
        return s, o_n

    xs = tuple(jnp.moveaxis(t, 2, 0) for t in (u, w, qk, q_dec, k_dec, g_last))
    s_final, o = lax.scan(step, s0, xs)
    o = jnp.moveaxis(o, 0, 2).reshape(bsz, nh, L, dv)
    return o, s_final


def _gated_deltanet(h_lat, h_ctx, w_in, conv_w, a_log, dt_bias, norm_g, w_out, need_ctx):
    f32 = jnp.float32
    hk = DN_HEADS * DN_DK
    hv = DN_HEADS * DN_DV
    n_qkv = 2 * hk + hv

    def project(h):
        bsz, L, _ = h.shape
        p = h @ w_in
        qkv = jax.nn.silu(_centred_dwconv(p[..., :n_qkv], conv_w))
        q = _l2norm(qkv[..., :hk].reshape(bsz, L, DN_HEADS, DN_DK)) * (DN_DK ** -0.5)
        k = _l2norm(qkv[..., hk:2 * hk].reshape(bsz, L, DN_HEADS, DN_DK))
        v = qkv[..., 2 * hk:].reshape(bsz, L, DN_HEADS, DN_DV).astype(f32)
        z = p[..., n_qkv:n_qkv + hv]
        off = n_qkv + hv
        a = p[..., off:off + 2 * DN_HEADS].reshape(bsz, L, 2, DN_HEADS).astype(f32)
        b = p[..., off + 2 * DN_HEADS:].reshape(bsz, L, 2, DN_HEADS).astype(f32)
        g = -jnp.exp(a_log.astype(f32)) * jax.nn.softplus(a + dt_bias.astype(f32))
        beta = jax.nn.sigmoid(b)
        heads = lambda t: jnp.transpose(t, (0, 2, 1, 3))
        dirs = lambda t: jnp.transpose(t, (2, 0, 3, 1))
        return heads(q), heads(k), heads(v), z, dirs(g), dirs(beta)

    qc, kc, vc, zc, gc, bc = project(h_ctx)
    ql, kl, vl, zl, gl, bl = project(h_lat)
    bsz = h_lat.shape[0]
    outs_lat, outs_ctx = [], []
    for d in range(2):
        s0 = jnp.zeros((bsz, DN_HEADS, DN_DK, DN_DV), f32)
        oc, s_ctx = _chunked_gated_delta(_flip_seq(qc, d), _flip_seq(kc, d), _flip_seq(vc, d),
                                         _flip_seq(gc[d], d), _flip_seq(bc[d], d), s0)
        ol, _ = _chunked_gated_delta(_flip_seq(ql, d), _flip_seq(kl, d), _flip_seq(vl, d),
                                     _flip_seq(gl[d], d), _flip_seq(bl[d], d), s_ctx)
        outs_ctx.append(_flip_seq(oc, d))
        outs_lat.append(_flip_seq(ol, d))

    def out(o, z):
        bsz_, _, L, _ = o.shape
        o = _rmsnorm(jnp.transpose(o, (0, 2, 1, 3)), norm_g.astype(f32))
        o = o.astype(z.dtype) * jax.nn.silu(z.reshape(bsz_, L, DN_HEADS, DN_DV))
        return o.reshape(bsz_, L, hv) @ w_out

    y_ctx = out(outs_ctx[0] + outs_ctx[1], zc) if need_ctx else None
    return out(outs_lat[0] + outs_lat[1], zl), y_ctx


def _diff_attention(h_lat, h_ctx, w_in, lam, subln_g, w_out, lambda_init, rope, need_ctx):
    f32 = jnp.float32
    hq = DA_HEADS * 2 * DA_DIM

    def project(h, use_rope):
        bsz, L, _ = h.shape
        p = h @ w_in
        q = p[..., :hq].reshape(bsz, L, DA_HEADS, 2, DA_DIM).transpose(3, 0, 2, 1, 4)
        k = p[..., hq:2 * hq].reshape(bsz, L, DA_HEADS, 2, DA_DIM).transpose(3, 0, 2, 1, 4)
        v = p[..., 2 * hq:].reshape(bsz, L, DA_HEADS, 2 * DA_DIM).transpose(0, 2, 1, 3)
        if use_rope:
            q = _apply_rope(q, rope[0], rope[1])
            k = _apply_rope(k, rope[0], rope[1])
        return q * (DA_DIM ** -0.5), k, v

    lam_f = lam.astype(f32)
    lam_val = (jnp.exp(jnp.sum(lam_f[0] * lam_f[1])) - jnp.exp(jnp.sum(lam_f[2] * lam_f[3]))
               + lambda_init)

    def attend(q, k, v):
        s = jnp.einsum('cbhqd,cbhkd->cbhqk', q, k).astype(f32)
        p = jax.nn.softmax(s, axis=-1)
        a = p[0] - lam_val * p[1]
        return jnp.einsum('bhqk,bhkv->bhqv', a, v.astype(f32))

    def out(o):
        bsz_, _, L, _ = o.shape
        o = _rmsnorm(jnp.transpose(o, (0, 2, 1, 3)), subln_g.astype(f32)) * (1.0 - lambda_init)
        return o.reshape(bsz_, L, hq).astype(h_lat.dtype) @ w_out

    qc, kc, vc = project(h_ctx, False)
    ql, kl, vl = project(h_lat, True)
    k_all = jnp.concatenate([kl, kc], axis=3)
    v_all = jnp.concatenate([vl, vc], axis=2)
    bsz, L = h_lat.shape[:2]
    nb = L // DA_QBLOCK
    qb = jnp.moveaxis(ql.reshape(2, bsz, DA_HEADS, nb, DA_QBLOCK, DA_DIM), 3, 0)
    ob = lax.map(lambda q_blk: attend(q_blk, k_all, v_all), qb)
    o_lat = jnp.moveaxis(ob, 0, 2).reshape(bsz, DA_HEADS, L, 2 * DA_DIM)
    y_ctx = out(attend(qc, kc, vc)) if need_ctx else None
    return out(o_lat), y_ctx


def setup_inputs(seed: int = 0) -> dict:
    key = jax.random.key(seed)
    ks = iter(jax.random.split(key, 32))
    f32 = jnp.float32
    D = D_MODEL
    hk = DN_HEADS * DN_DK
    hv = DN_HEADS * DN_DV
    hq = DA_HEADS * 2 * DA_DIM

    def nrm(shape, scale):
        return jax.random.normal(next(ks), shape, f32) * scale

    inputs = {}
    inputs['x'] = nrm((BATCH, SEQ, D), 1.0)
    inputs['c'] = nrm((BATCH, D), 1.0)
    inputs['ctx'] = nrm((BATCH, CTX_LEN, D), 1.0)
    inputs['c_ctx'] = nrm((D,), 1.0)
    inputs['ada_w'] = nrm((DEPTH, D, 6 * D), D ** -0.5)
    inputs['ada_b'] = nrm((DEPTH, 6 * D), 0.01)
    inputs['ln1_g'] = 1.0 + nrm((DEPTH, D), 0.02)
    inputs['ln1_b'] = nrm((DEPTH, D), 0.02)
    inputs['ln2_g'] = 1.0 + nrm((DEPTH, D), 0.02)
    inputs['ln2_b'] = nrm((DEPTH, D), 0.02)
    inputs['ffn_w_gate'] = nrm((DEPTH, D, FFN_HIDDEN), D ** -0.5)
    inputs['ffn_w_up'] = nrm((DEPTH, D, FFN_HIDDEN), D ** -0.5)
    inputs['ffn_w_down'] = nrm((DEPTH, FFN_HIDDEN, D), FFN_HIDDEN ** -0.5 * DEEPNORM_BETA)
    inputs['dn_w_in'] = jnp.concatenate([
        nrm((N_DN_LAYERS, D, 2 * hk + 2 * hv), D ** -0.5),
        nrm((N_DN_LAYERS, D, 2 * DN_HEADS), 0.1 * D ** -0.5),
        nrm((N_DN_LAYERS, D, 2 * DN_HEADS), D ** -0.5),
    ], axis=-1)
    inputs['dn_conv'] = nrm((N_DN_LAYERS, DN_CONV, 2 * hk + hv), DN_CONV ** -0.5)
    inputs['dn_a_log'] = jnp.log(jax.random.uniform(next(ks), (N_DN_LAYERS, 2, DN_HEADS), f32, 1.0, 16.0))
    dt = jnp.exp(jax.random.uniform(next(ks), (N_DN_LAYERS, 2, DN_HEADS), f32,
                                    math.log(1e-3), math.log(1e-1)))
    inputs['dn_dt_bias'] = dt + jnp.log(-jnp.expm1(-dt))
    inputs['dn_norm_g'] = 1.0 + nrm((N_DN_LAYERS, DN_DV), 0.02)
    inputs['dn_w_out'] = nrm((N_DN_LAYERS, hv, D), hv ** -0.5 * DEEPNORM_BETA)
    inputs['da_w_in'] = nrm((N_DA_LAYERS, D, 3 * hq), D ** -0.5)
    inputs['da_lambda'] = nrm((N_DA_LAYERS, 4, DA_DIM), 0.1)
    inputs['da_subln_g'] = 1.0 + nrm((N_DA_LAYERS, 2 * DA_DIM), 0.02)
    inputs['da_w_out'] = nrm((N_DA_LAYERS, hq, D), hq ** -0.5 * DEEPNORM_BETA)
    return inputs


def reference(x, c, ctx, c_ctx, ada_w, ada_b, ln1_g, ln1_b, ln2_g, ln2_b,
              ffn_w_gate, ffn_w_up, ffn_w_down,
              dn_w_in, dn_conv, dn_a_log, dn_dt_bias, dn_norm_g, dn_w_out,
              da_w_in, da_lambda, da_subln_g, da_w_out):
    L = x.shape[1]
    rows = L // GRID_W
    rope = _axial_rope(rows, x.dtype)
    s_lat = jax.nn.silu(c)
    s_ctx = jax.nn.silu(c_ctx)
    for i in range(DEPTH):
        last = i == DEPTH - 1
        sh1, sc1, gt1, sh2, sc2, gt2 = jnp.split((s_lat @ ada_w[i] + ada_b[i])[:, None, :], 6, axis=-1)
        sh1c, sc1c, gt1c, sh2c, sc2c, gt2c = jnp.split(s_ctx @ ada_w[i] + ada_b[i], 6, axis=-1)
        h = x * (1.0 + sc1) + sh1
        hc = ctx * (1.0 + sc1c) + sh1c
        j = i // N_MIXERS
        if i % N_MIXERS == 0:
            y, yc = _gated_deltanet(h, hc, dn_w_in[j], dn_conv[j], dn_a_log[j], dn_dt_bias[j],
                                    dn_norm_g[j], dn_w_out[j], not last)
        else:
            y, yc = _diff_attention(h, hc, da_w_in[j], da_lambda[j], da_subln_g[j], da_w_out[j],
                                    0.8 - 0.6 * math.exp(-0.3 * i), rope, not last)
        x = _layernorm(DEEPNORM_ALPHA * x + gt1 * y, ln1_g[i], ln1_b[i])
        x = _layernorm(DEEPNORM_ALPHA * x
                       + gt2 * _swiglu(x * (1.0 + sc2) + sh2, ffn_w_gate[i], ffn_w_up[i], ffn_w_down[i]),
                       ln2_g[i], ln2_b[i])
        if not last:
            ctx = _layernorm(DEEPNORM_ALPHA * ctx + gt1c * yc, ln1_g[i], ln1_b[i])
            ctx = _layernorm(DEEPNORM_ALPHA * ctx
                             + gt2c * _swiglu(ctx * (1.0 + sc2c) + sh2c, ffn_w_gate[i], ffn_w_up[i], ffn_w_down[i]),
                             ln2_g[i], ln2_b[i])
    return x
```

```python
import math
from contextlib import ExitStack

import numpy as np
import concourse.bass as bass
import concourse.mybir as mybir
from concourse.bass_utils import run_bass_kernel_spmd

F32 = mybir.dt.float32
BF16 = mybir.dt.bfloat16
AF = mybir.ActivationFunctionType
ALU = mybir.AluOpType

D = 1024
NCH = 8
TL = 2048
TC = 256
T = TL + TC
FF = 2816
NFF = 22
ALPHA = (2.0 * 2) ** 0.25
LN_EPS = 1e-5
RMS_EPS = 1e-6
NB = 256
BLOCKS = [(i * NB, NB) for i in range(T // NB)]


def blk_set(t0):
    return 0 if t0 < TL else 1


class K:
    def __init__(self, nc, es):
        self.nc = nc
        self.es = es
        self.engs = {"pe": nc.tensor, "act": nc.scalar, "dve": nc.vector,
                     "pool": nc.gpsimd, "sp": nc.sync}
        self.esem = {}
        for e in self.engs:
            self.esem[e] = es.enter_context(nc.semaphore("sem_" + e))
        self.ecnt = {e: 0 for e in self.engs}
        self.seen = {e: {} for e in self.engs}
        self.sems = {}
        for e in self.engs:
            self.sems["sem_" + e] = self.esem[e]
        self.lastw = {}
        self.rd = {}
        self.dsem = {}
        self.ninst = 0
        self.nwait = 0
        self.fence_ev = {}

    def fence(self):
        ev = {}
        for e in self.engs:
            if self.ecnt[e]:
                ev["sem_" + e] = self.ecnt[e]
        for nm, cnt in self.dsem.values():
            if cnt:
                ev[nm] = cnt
        self.fence_ev = ev

    def sb(self, name, shape, dt):
        return self.es.enter_context(self.nc.sbuf_tensor(name, list(shape), dt))

    def ps(self, name, shape, dt=F32):
        return self.es.enter_context(self.nc.psum_tensor(name, list(shape), dt))

    def dram(self, name, shape, dt, kind="Internal"):
        return self.nc.dram_tensor(name, list(shape), dt, kind=kind).ap()

    def _emit(self, e, fn, rd, wr, dma=False, append=False, noinc=False):
        need = {}

        def add(ev):
            if ev is None:
                return
            s, v = ev
            if e == "pe" and s == "sem_pe":
                return
            if need.get(s, 0) < v:
                need[s] = v

        for r in rd:
            add(self.lastw.get(r))
            if isinstance(r, tuple) and r[0] == "ps":
                for s_, v_ in self.rd.get(r, {}).items():
                    if s_ != "sem_" + e:
                        add((s_, v_))
        if wr:
            for s_, v_ in self.fence_ev.items():
                if not (s_ == "sem_" + e and e != "pe"):
                    add((s_, v_)) if not (e == "pe" and s_ == "sem_pe") else None
                else:
                    add((s_, v_))
        for w in wr:
            lw = self.lastw.get(w)
            if not (append and dma and lw is not None and w in self.dsem and lw[0] == self.dsem[w][0]):
                add(lw)
            for s, v in self.rd.get(w, {}).items():
                add((s, v))
        eng = self.engs[e]
        seen = self.seen[e]
        todo = [(s, v) for s, v in need.items() if seen.get(s, 0) < v]
        attach = None
        if todo and not dma:
            attach = todo.pop()
        for s, v in todo:
            eng.wait_ge(self.sems[s], v)
            seen[s] = v
            self.nwait += 1
        ins = fn(eng)
        if attach is not None:
            ins._wait_ge(self.sems[attach[0]], attach[1])
            seen[attach[0]] = attach[1]
        self.ninst += 1
        if dma:
            w0 = wr[0]
            if w0 not in self.dsem:
                nm = "dsem%d" % len(self.dsem)
                self.sems[nm] = self.es.enter_context(self.nc.semaphore(nm))
                self.dsem[w0] = [nm, 0]
            ds = self.dsem[w0]
            ds[1] += 16
            ins.then_inc(self.sems[ds[0]], 16)
            ev = (ds[0], ds[1])
        elif noinc:
            assert e == "pe"
            ev = ("sem_" + e, self.ecnt[e] + 1)
        else:
            self.ecnt[e] += 1
            ins.then_inc(self.esem[e], 1)
            ev = ("sem_" + e, self.ecnt[e])
        for r in rd:
            d = self.rd.setdefault(r, {})
            if d.get(ev[0], 0) < ev[1]:
                d[ev[0]] = ev[1]
        for w in wr:
            self.lastw[w] = ev
            self.rd[w] = {}
        return ev

    def wait_all(self, e, keys):
        eng = self.engs[e]
        for k in keys:
            ev = self.lastw.get(k)
            if ev is None:
                continue
            eng.wait_ge(self.sems[ev[0]], ev[1])

    def dma(self, q, out, in_, rd, wr, append=False):
        return self._emit(q, lambda g: g.dma_start(out=out, in_=in_), rd, wr, dma=True, append=append)

    def mm(self, out, lhsT, rhs, start, stop, rd, wr):
        return self._emit("pe", lambda g: g.matmul(out, lhsT, rhs, start=start, stop=stop), rd, wr, noinc=not stop)

    def tr(self, out, in_, ident, rd, wr):
        return self._emit("pe", lambda g: g.transpose(out, in_, ident), rd, wr)

    def act(self, out, in_, func, rd, wr, bias=None, scale=None, e="act"):
        kw = {}
        if bias is not None:
            kw["bias"] = bias
        if scale is not None:
            kw["scale"] = scale
        return self._emit(e, lambda g: g.activation(out=out, in_=in_, func=func, **kw), rd, wr)

    def ts(self, e, out, in0, s1, s2, op0, op1, rd, wr):
        if s2 is None:
            return self._emit(e, lambda g: g.tensor_scalar(out=out, in0=in0, scalar1=s1, scalar2=None, op0=op0), rd, wr)
        return self._emit(e, lambda g: g.tensor_scalar(out=out, in0=in0, scalar1=s1, scalar2=s2, op0=op0, op1=op1), rd, wr)

    def tt(self, e, out, in0, in1, op, rd, wr):
        return self._emit(e, lambda g: g.tensor_tensor(out=out, in0=in0, in1=in1, op=op), rd, wr)

    def stt(self, e, out, in0, scalar, in1, op0, op1, rd, wr):
        return self._emit(e, lambda g: g.scalar_tensor_tensor(out=out, in0=in0, scalar=scalar, in1=in1, op0=op0, op1=op1), rd, wr)

    def copy(self, e, out, in_, rd, wr):
        if e == "act":
            return self._emit(e, lambda g: g.activation(out=out, in_=in_, func=AF.Copy), rd, wr)
        return self._emit(e, lambda g: g.tensor_copy(out=out, in_=in_), rd, wr)

    def memset(self, e, ap, val, wr):
        return self._emit(e, lambda g: g.memset(ap, val), [], wr)

    def recip(self, out, in_, rd, wr):
        return self._emit("dve", lambda g: g.reciprocal(out=out, in_=in_), rd, wr)


VEC_LAYOUT = {}


def _vec_layout():
    if VEC_LAYOUT:
        return VEC_LAYOUT
    off = 0

    def add(name, n):
        nonlocal off
        VEC_LAYOUT[name] = (off, n)
        off += n

    add("ada_b", 2 * 48)
    for nm in ("ln1_g", "ln1_b", "ln2_g", "ln2_b"):
        add(nm, 2 * 8)
    add("dn_conv", 24 * 5)
    add("dn_norm_g", 1)
    add("da_subln_g", 1)
    add("dn_a_log", 2)
    add("dn_dt_bias", 2)
    add("da_lambda", 4)
    VEC_LAYOUT["_total"] = (off, 0)
    return VEC_LAYOUT


def pack_vecs(inp):
    L = _vec_layout()
    tot = L["_total"][0]
    v = np.zeros((128, tot), np.float32)

    def put(name, arr):
        o, n = L[name]
        assert arr.shape == (arr.shape[0], n), (name, arr.shape, n)
        v[: arr.shape[0], o:o + n] = arr

    put("ada_b", np.concatenate([inp["ada_b"][i].reshape(48, 128).T for i in range(2)], axis=1))
    for nm in ("ln1_g", "ln1_b", "ln2_g", "ln2_b"):
        put(nm, np.concatenate([inp[nm][i].reshape(8, 128).T for i in range(2)], axis=1))
    cv = inp["dn_conv"][0]
    put("dn_conv", cv.reshape(5, 24, 128).transpose(2, 1, 0).reshape(128, 120))
    put("dn_norm_g", inp["dn_norm_g"][0].reshape(128, 1))
    put("da_subln_g", inp["da_subln_g"][0].reshape(128, 1))
    put("dn_a_log", inp["dn_a_log"][0].T.copy())
    put("dn_dt_bias", inp["dn_dt_bias"][0].T.copy())
    put("da_lambda", inp["da_lambda"][0].T.copy())
    return v


def make_consts():
    c = {}
    c["ident"] = np.eye(128, dtype=np.float32)
    c["onesD"] = np.full((128, 128), 1.0 / D, np.float32)
    c["ones"] = np.ones((128, 128), np.float32)
    quarter = 16
    inv_freq = 10000.0 ** (-np.arange(quarter, dtype=np.float32) / quarter)
    rows = TL // 64
    row = np.repeat(np.arange(rows, dtype=np.float32), 64)
    col = np.tile(np.arange(64, dtype=np.float32), rows)
    ang_r = row[:, None] * inv_freq
    ang_c = col[:, None] * inv_freq
    ang = np.concatenate([ang_r, ang_r, ang_c, ang_c], axis=-1)
    c["ropecos"] = np.ascontiguousarray(np.concatenate([np.cos(ang).T, np.cos(ang).T], axis=0), dtype=np.float32)
    c["ropesin"] = np.ascontiguousarray(np.concatenate([np.sin(ang).T, np.sin(ang).T], axis=0), dtype=np.float32)
    R = np.zeros((128, 128), np.float32)
    for comp in range(2):
        for half in range(2):
            b0 = comp * 64 + half * 32
            for d in range(16):
                R[b0 + 16 + d, b0 + d] = -1.0
                R[b0 + d, b0 + 16 + d] = 1.0
    c["roperot"] = R
    sel = np.zeros((8, 8, 128), np.float32)
    for r in range(8):
        sel[r, r, :] = 1.0
    c["dn_sel"] = sel.reshape(8, 8 * 128)
    p = np.arange(64)[:, None]
    f = np.arange(64)[None, :]
    BIG = 30000.0
    masks = np.stack([
        np.where(f >= p, 0.0, -BIG),
        np.where(f > p, 0.0, -BIG),
        np.where(f < p, 0.0, BIG),
        np.where(f <= p, 0.0, -BIG),
        np.where(f < p, 0.0, -BIG),
        np.where(f > p, 0.0, BIG),
    ], axis=1).astype(np.float32)
    c["dn_masks"] = np.ascontiguousarray(masks.reshape(64, 6 * 64))
    c["dn_masks2"] = np.ascontiguousarray(np.concatenate([masks[:, 0:3, :], masks[:, 3:6, :]], axis=0).reshape(128, 3 * 64))
    c["dn_ident2"] = np.ascontiguousarray(np.concatenate([np.eye(64), np.eye(64)], axis=0).astype(np.float32))
    return c


CONST_SHAPES = {"ident": [128, 128], "onesD": [128, 128], "ones": [128, 128],
                "ropecos": [128, TL], "ropesin": [128, TL], "roperot": [128, 128],
                "dn_sel": [8, 8 * 128], "dn_masks": [64, 6 * 64],
                "dn_masks2": [128, 3 * 64], "dn_ident2": [128, 64]}


class Prog:
    def __init__(self, phases, ext=()):
        self.phases = phases
        self.ext = ext
        self.nc = bass.Bass("TRN2", target_bir_lowering=False)
        self.es = ExitStack()
        self.k = K(self.nc, self.es)
        self.tapkeys = []

    def _sbt(self, name, shape, dt):
        self._uid = getattr(self, "_uid", 0) + 1
        return self.nc.sbuf_tensor("%s_u%d" % (name, self._uid), shape, dt)

    def build(self):
        nc, k = self.nc, self.k
        L = _vec_layout()
        NV = L["_total"][0]
        self.xin = k.dram("xin", [D, T], F32, kind="ExternalInput")
        self.cvec_d = k.dram("cvec", [128, 16], F32, kind="ExternalInput")
        self.vecs_d = k.dram("vecs", [128, NV], F32, kind="ExternalInput")
        self.const_d = {nm: k.dram(nm, shp, F32, kind="ExternalInput") for nm, shp in CONST_SHAPES.items()}
        self.ada_w = k.dram("ada_w", [2, D, 6 * D], F32, kind="ExternalInput")
        self.w_gate = k.dram("ffn_w_gate", [2, D, FF], F32, kind="ExternalInput")
        self.w_up = k.dram("ffn_w_up", [2, D, FF], F32, kind="ExternalInput")
        self.w_down = k.dram("ffn_w_down", [2, FF, D], F32, kind="ExternalInput")
        self.da_w_in = k.dram("da_w_in", [D, 3 * D], F32, kind="ExternalInput")
        self.da_w_out = k.dram("da_w_out", [D, D], F32, kind="ExternalInput")
        self.dn_w_in = k.dram("dn_w_in", [D, 4128], F32, kind="ExternalInput")
        self.dn_w_out = k.dram("dn_w_out", [D, D], F32, kind="ExternalInput")
        self.outT = k.dram("outT", [D, TL], F32, kind="ExternalOutput")
        self.xs = [k.dram(nm, [D, T], F32, kind=("ExternalOutput" if nm in self.ext else "Internal"))
                   for nm in ("xs0", "xs1")]

        self.vecs = k.sb("vecs_sb", [128, NV], F32)
        self.ident = k.sb("ident_sb", [128, 128], F32)
        self.identb = k.sb("identb_sb", [128, 128], BF16)
        self.onesD = k.sb("onesD_sb", [128, 128], F32)
        self.ones = k.sb("ones_sb", [128, 128], F32)
        self.onesb = k.sb("onesb_sb", [128, 128], BF16)
        self.svec = k.sb("svec", [128, 16], F32)
        self.mod = k.sb("mod", [128, 96], F32)
        self.modp = k.sb("modp", [128, 96], F32)
        self.psb = [k.ps("psb%d" % i, [128, 512], F32) for i in range(8)]
        self.psbf = [self.psb[6][:, :].bitcast(BF16), self.psb[7][:, :].bitcast(BF16)]

        k.dma("sp", self.vecs[:, :], self.vecs_d[:, :], [], ["vecs"])
        k.dma("sp", self.ident[:, :], self.const_d["ident"][:, :], [], ["ident"])
        k.dma("sp", self.onesD[:, :], self.const_d["onesD"][:, :], [], ["onesD"])
        k.dma("sp", self.ones[:, :], self.const_d["ones"][:, :], [], ["ones"])
        k.dma("sp", self.svec[:, :], self.cvec_d[:, :], [], ["svec"])
        k.act(self.svec[:, :], self.svec[:, :], AF.Silu, ["svec"], ["svec"])
        k.copy("dve", self.identb[:, :], self.ident[:, :], ["ident"], ["identb"])
        k.copy("dve", self.onesb[:, :], self.ones[:, :], ["ones"], ["onesb"])

        cur = self.xin
        for ph in self.phases:
            kind, layer = ph
            if kind == "ada":
                self.phase_ada(layer)
            elif kind == "ffn":
                last = (layer == 1)
                dst = self.outT if last else self.xs[0]
                self.phase_ffn(layer, cur, dst, last)
                cur = dst
            elif kind == "da":
                self.phase_da(layer, cur, self.xs[1])
                cur = self.xs[1]
            elif kind == "dn":
                self.phase_dn(layer, cur, self.xs[1])
                cur = self.xs[1]
        keys = list(self.tapkeys)
        for nm in ("out", "xs0", "xs1"):
            keys += [(nm, b) for b in range(len(BLOCKS))]
        k.wait_all("sp", keys)
        return nc

    def tap(self, name, ap, shape, key, dt=F32):
        if name not in self.ext:
            return
        d = self.k.dram(name, list(shape), dt, kind="ExternalOutput")
        self.k.dma("sp", d, ap, [key] if not isinstance(key, list) else key, [("tap", name)])
        self.tapkeys.append(("tap", name))

    def vcol(self, name, idx):
        o, n = _vec_layout()[name]
        return self.vecs[:, o + idx:o + idx + 1]

    def modcol(self, m, c, s, plus1=False):
        j = m * 8 + c
        t = self.modp if plus1 else self.mod
        return t[:, j * 2 + s:j * 2 + s + 1]

    def phase_ada(self, layer):
        k = self.k
        k.fence()
        GW = 768
        NG = 6 * D // GW
        with ExitStack() as es:
            wbuf = [es.enter_context(self._sbt("adaw%d" % i, [128, 8, GW], F32)) for i in range(2)]
            ps = self.psb[0]
            src = self.ada_w[layer].rearrange("(kc p) n -> p kc n", p=128)
            for g in range(NG):
                wb = wbuf[g % 2]
                key = ("adaw", g % 2)
                k.dma("sp", wb[:, :, :], src[:, :, g * GW:(g + 1) * GW], [], [key])
                for jj in range(GW // 128):
                    j = g * (GW // 128) + jj
                    for kc in range(8):
                        k.mm(ps[:, j * 2:j * 2 + 2], wb[:, kc, jj * 128:(jj + 1) * 128],
                             self.svec[:, kc * 2:kc * 2 + 2], kc == 0, kc == 7,
                             [key, "svec"], [("ps", 0)])
            o, n = _vec_layout()["ada_b"]
            bcol = self.vecs[:, o + layer * 48:o + layer * 48 + 48]
            k.tt("dve", self.mod[:, :].rearrange("p (j s) -> p j s", s=2),
                 ps[:, 0:96].rearrange("p (j s) -> p j s", s=2),
                 bcol.unsqueeze(2).to_broadcast([128, 48, 2]), ALU.add,
                 [("ps", 0), "vecs"], ["mod"])
            k.ts("dve", self.modp[:, :], self.mod[:, :], 1.0, None, ALU.add, None, ["mod"], ["modp"])
            self.tap("tap_mod%d" % layer, self.mod[:, :], [128, 96], "mod")

    def ln_block(self, r, n, gname, bname, layer, outb, rkey, okey, tmp, extra_wr=()):
        k = self.k
        sq, mean, var = tmp["sq"], tmp["mean"], tmp["var"]
        inplace = outb is None
        if inplace:
            outb = sq
        psm, psv = self.psb[6], self.psb[7]
        for c in range(8):
            k.act(sq[:, c, :n], r[:, c, :n], AF.Square, [rkey], [("ln_sq", c)] + (list(extra_wr) if c == 0 else []))
        for c in range(8):
            k.mm(psm[:, :n], self.onesD[:, :], r[:, c, :n], c == 0, c == 7, ["onesD", rkey], [("ps", 6)])
        for c in range(8):
            k.mm(psv[:, :n], self.onesD[:, :], sq[:, c, :n], c == 0, c == 7, ["onesD", ("ln_sq", c)], [("ps", 7)])
        k.copy("act", mean[:, :n], psm[:, :n], [("ps", 6)], ["ln_mean"])
        k.tt("dve", var[:, :n], mean[:, :n], mean[:, :n], ALU.mult, ["ln_mean"], ["ln_var"])
        k.tt("dve", var[:, :n], psv[:, :n], var[:, :n], ALU.subtract, [("ps", 7), "ln_var"], ["ln_var"])
        k.ts("dve", var[:, :n], var[:, :n], LN_EPS, None, ALU.add, None, ["ln_var"], ["ln_var"])
        k.act(var[:, :n], var[:, :n], AF.Sqrt, ["ln_var"], ["ln_var"])
        k.recip(var[:, :n], var[:, :n], ["ln_var"], ["ln_var"])
        go, _ = _vec_layout()[gname]
        bo, _ = _vec_layout()[bname]
        for c in range(8):
            e = "dve" if c % 2 == 0 else "pool"
            k.tt(e, sq[:, c, :n], r[:, c, :n], mean[:, :n], ALU.subtract, [rkey, "ln_mean"], [("ln_sq", c)])
            k.tt(e, sq[:, c, :n], sq[:, c, :n], var[:, :n], ALU.mult, [("ln_sq", c), "ln_var"], [("ln_sq", c)])
            k.ts(e, outb[:, c, :n], sq[:, c, :n],
                 self.vecs[:, go + layer * 8 + c:go + layer * 8 + c + 1],
                 self.vecs[:, bo + layer * 8 + c:bo + layer * 8 + c + 1],
                 ALU.mult, ALU.add, [("ln_sq", c), "vecs"], [("ln_sq", c)] if inplace else [okey])
        return [("ln_sq", c) for c in range(8)]

    def phase_ffn(self, layer, src, dst, last):
        k = self.k
        k.fence()
        FB = 512
        fblocks = [(i * FB, FB) for i in range(TL // FB)] + ([] if last else [(TL, TC)])
        nblocks = len(fblocks)
        with ExitStack() as es:
            def sb(name, shape, dt):
                return es.enter_context(self._sbt(name, list(shape), dt))
            wg = sb("wg", [128, 8, FF], BF16)
            wu = sb("wu", [128, 8, FF], BF16)
            wd = sb("wd", [128, NFF, D], BF16)
            xb = [sb("xb%d" % i, [128, 8, FB], F32) for i in range(2)]
            h2 = sb("h2", [128, 8, FB], BF16)
            abuf = sb("abuf", [128, NFF * FB // 2], F32)
            a = abuf[:, :].bitcast(BF16).rearrange("p (j n) -> p j n", n=FB)
            lnsq = abuf[:, 0:8 * FB].rearrange("p (c n) -> p c n", n=FB)
            sg = [sb("sg%d" % i, [128, FB], F32) for i in range(2)]
            tmp = {"sq": lnsq, "mean": sb("lnmean", [128, FB], F32), "var": sb("lnvar", [128, FB], F32)}
            akeys = [("a", j) for j in range(NFF)]
            for g_ in range(2):
                c0_, c1_ = g_ * (FF // 2), (g_ + 1) * (FF // 2)
                for kc in range(8):
                    k.dma("pool", wg[:, kc, c0_:c1_], self.w_gate[layer, kc * 128:(kc + 1) * 128, c0_:c1_], [], [("wg", g_)], append=True)
                    k.dma("pool", wu[:, kc, c0_:c1_], self.w_up[layer, kc * 128:(kc + 1) * 128, c0_:c1_], [], [("wu", g_)], append=True)
            for j in range(NFF):
                k.dma("pool", wd[:, j, :], self.w_down[layer, j * 128:(j + 1) * 128, :], [], [("wd", j // 6)], append=True)
            srcv = src.rearrange("(c p) t -> p c t", p=128)
            dstv = dst.rearrange("(c p) t -> p c t", p=128)
            dname = "out" if last else "xs0"
            k.dma("sp", xb[0][:, :, :fblocks[0][1]], srcv[:, :, 0:fblocks[0][1]], [], [("xb", 0)])
            for bi in range(nblocks):
                t0, n = fblocks[bi]
                s = blk_set(t0)
                x = xb[bi % 2]
                xk = ("xb", bi % 2)
                if bi + 1 < nblocks:
                    t1, n1 = fblocks[bi + 1]
                    k.dma("sp", xb[(bi + 1) % 2][:, :, :n1], srcv[:, :, t1:t1 + n1], [], [("xb", (bi + 1) % 2)])
                for c in range(8):
                    e = "dve" if c % 2 == 0 else "pool"
                    k.ts(e, h2[:, c, :n], x[:, c, :n], self.modcol(4, c, s, True), self.modcol(3, c, s),
                         ALU.mult, ALU.add, [xk, "mod", "modp"], [("h2", c)])
                for j in range(NFF):
                    pg, pu = self.psb[(2 * j) % 4], self.psb[(2 * j + 1) % 4]
                    kg, ku = ("ps", (2 * j) % 4), ("ps", (2 * j + 1) % 4)
                    for kc in range(8):
                        k.mm(pg[:, :n], wg[:, kc, j * 128:(j + 1) * 128], h2[:, kc, :n], kc == 0, kc == 7,
                             [("wg", j // 11), ("h2", kc)], [kg])
                    for kc in range(8):
                        k.mm(pu[:, :n], wu[:, kc, j * 128:(j + 1) * 128], h2[:, kc, :n], kc == 0, kc == 7,
                             [("wu", j // 11), ("h2", kc)], [ku])
                    sgb = sg[j % 2]
                    k.act(sgb[:, :n], pg[:, :n], AF.Silu, [kg], [("sg", j % 2)])
                    k.tt("dve", a[:, j, :n], sgb[:, :n], pu[:, :n], ALU.mult, [("sg", j % 2), ku], [("a", j)])
                for c in range(8):
                    k.ts("pool", x[:, c, :n], x[:, c, :n], ALPHA, None, ALU.mult, None, [xk, ("h2", c)], [xk])
                for c in range(8):
                    pd = self.psb[4 + c % 2]
                    kd = ("ps", 4 + c % 2)
                    for j in range(NFF):
                        k.mm(pd[:, :n], wd[:, j, c * 128:(c + 1) * 128], a[:, j, :n], j == 0, j == NFF - 1,
                             [("wd", j // 6), ("a", j)], [kd])
                    k.stt("dve", x[:, c, :n], pd[:, :n], self.modcol(5, c, s), x[:, c, :n], ALU.mult, ALU.add,
                          [kd, "mod", xk], [xk])
                okeys = self.ln_block(x, n, "ln2_g", "ln2_b", layer, None, xk, None, tmp, extra_wr=akeys)
                k.dma("sp", dstv[:, :, t0:t0 + n], lnsq[:, :, :n], okeys + akeys, [(dname, bi)])

    def mixer_out(self, es, layer, w_out_d, og, src, dst, nblocks, dname, ogkeys, og_dram=None):
        k = self.k

        def sb(name, shape, dt):
            return es.enter_context(self._sbt(name, list(shape), dt))
        MB = 512
        mblocks = [(i * MB, MB) for i in range(TL // MB)] + ([(TL, TC)] if nblocks == 9 else [])
        nblocks = len(mblocks)
        wo = sb("wo", [128, 8, D], BF16)
        xb = [sb("mo_xb%d" % i, [128, 8, MB], F32) for i in range(2)]
        tmp = {"sq": sb("mo_lnsq", [128, 8, MB], F32), "mean": sb("mo_lnmean", [128, MB], F32),
               "var": sb("mo_lnvar", [128, MB], F32)}
        for kc in range(8):
            k.dma("pool", wo[:, kc, :], w_out_d[kc * 128:(kc + 1) * 128, :], [], ["wo"], append=True)
        srcv = src.rearrange("(c p) t -> p c t", p=128)
        dstv = dst.rearrange("(c p) t -> p c t", p=128)
        n0 = mblocks[0][1]
        k.dma("sp", xb[0][:, :, :n0], srcv[:, :, 0:n0], [], [("mo_xb", 0)])
        if og_dram is not None:
            ogb = [sb("mo_ogb%d" % i, [128, 8, MB], BF16) for i in range(2)]
            ogv = og_dram.rearrange("h p t -> p h t")
            k.dma("sp", ogb[0][:, :, :n0], ogv[:, :, 0:n0], ["dn_og"], [("mo_ogb", 0)])
        for bi in range(nblocks):
            t0, n = mblocks[bi]
            s_ = blk_set(t0)
            x = xb[bi % 2]
            xk = ("mo_xb", bi % 2)
            if bi + 1 < nblocks:
                t1, n1 = mblocks[bi + 1]
                k.dma("sp", xb[(bi + 1) % 2][:, :, :n1], srcv[:, :, t1:t1 + n1], [], [("mo_xb", (bi + 1) % 2)])
                if og_dram is not None:
                    k.dma("sp", ogb[(bi + 1) % 2][:, :, :n1], ogv[:, :, t1:t1 + n1], ["dn_og"], [("mo_ogb", (bi + 1) % 2)])
            if og_dram is not None:
                og = ogb[bi % 2]
                ogoff = 0
                okf = lambda h, bi_=bi: [("mo_ogb", bi_ % 2)]
            else:
                ogoff = t0
                okf = lambda h, t0_=t0, n_=n: [k_ for i_ in range(n_ // 256) for k_ in ogkeys(h, t0_ // 256 + i_)]
            for c in range(8):
                k.ts("pool", x[:, c, :n], x[:, c, :n], ALPHA, None, ALU.mult, None, [xk], [xk])
            for c in range(8):
                pd = self.psb[4 + c % 2]
                kd = ("ps", 4 + c % 2)
                for h in range(8):
                    k.mm(pd[:, :n], wo[:, h, c * 128:(c + 1) * 128], og[:, h, ogoff:ogoff + n], h == 0, h == 7,
                         ["wo"] + okf(h), [kd])
                k.stt("dve", x[:, c, :n], pd[:, :n], self.modcol(2, c, s_), x[:, c, :n], ALU.mult, ALU.add,
                      [kd, "mod", xk], [xk])
            okeys = self.ln_block(x, n, "ln1_g", "ln1_b", layer, None, xk, None, tmp)
            k.dma("sp", dstv[:, :, t0:t0 + n], tmp["sq"][:, :, :n], okeys, [(dname, bi)])

    def load_h(self, es, src, hT, nblocks):
        k = self.k
        xb = [es.enter_context(self._sbt("lh_xb%d" % i, [128, 8, NB], F32)) for i in range(2)]
        srcv = src.rearrange("(c p) t -> p c t", p=128)
        k.dma("sp", xb[0][:, :, :], srcv[:, :, 0:NB], [], [("lh_xb", 0)])
        for bi in range(nblocks):
            t0, n = BLOCKS[bi]
            s_ = blk_set(t0)
            if bi + 1 < nblocks:
                t1, _ = BLOCKS[bi + 1]
                k.dma("sp", xb[(bi + 1) % 2][:, :, :], srcv[:, :, t1:t1 + NB], [], [("lh_xb", (bi + 1) % 2)])
            for c in range(8):
                e = "dve" if c % 2 == 0 else "pool"
                k.ts(e, hT[:, c, t0:t0 + n], xb[bi % 2][:, c, :], self.modcol(1, c, s_, True), self.modcol(0, c, s_),
                     ALU.mult, ALU.add, [("lh_xb", bi % 2), "mod", "modp"], [("hT", bi)])

    def phase_da(self, layer, src, dst):
        k = self.k
        k.fence()
        lam_init = 0.8 - 0.6 * math.exp(-0.3 * layer)
        NKT = T // 128
        QB = 512
        with ExitStack() as es_outer:
            def sbo(name, shape, dt):
                return es_outer.enter_context(self._sbt(name, list(shape), dt))
            qT = sbo("da_qT", [128, 8, TL], BF16)
            og = qT
            with ExitStack() as es_mid:
                def sbm(name, shape, dt):
                    return es_mid.enter_context(self._sbt(name, list(shape), dt))
                kT = sbm("da_kT", [128, 8, T], BF16)
                vt = sbm("da_vt", [128, NKT, 8, 130], BF16)
                lam = sbm("da_lam", [128, 4], F32)
                lo, _ = _vec_layout()["da_lambda"]
                k.tt("dve", lam[0:64, 0:1], self.vecs[0:64, lo:lo + 1], self.vecs[0:64, lo + 1:lo + 2], ALU.mult, ["vecs"], ["lam"])
                k.tt("dve", lam[0:64, 1:2], self.vecs[0:64, lo + 2:lo + 3], self.vecs[0:64, lo + 3:lo + 4], ALU.mult, ["vecs", "lam"], ["lam"])
                pl = self.psb[0]
                k.mm(pl[:, 0:2], self.ones[0:64, :], lam[0:64, 0:2], True, True, ["ones", "lam"], [("ps", 0)])
                k.act(lam[:, 2:4], pl[:, 0:2], AF.Exp, [("ps", 0), "lam"], ["lam"])
                k.tt("dve", lam[:, 0:1], lam[:, 2:3], lam[:, 3:4], ALU.subtract, ["lam"], ["lam"])
                k.ts("dve", lam[:, 1:2], lam[:, 0:1], lam_init, -1.0, ALU.add, ALU.mult, ["lam"], ["lam"])
                k.memset("pool", vt[:, :, :, 128:130], 1.0, ["vt_ones"])
                self.tap("tap_lam", lam[:, :], [128, 4], ["lam", "vt_ones"])
                if getattr(self, "stop_after", "") == "lam":
                    return
                with ExitStack() as es:
                    def sb(name, shape, dt):
                        return es.enter_context(self._sbt(name, list(shape), dt))
                    hT = sb("da_hT", [128, 8, T], BF16)
                    w = sb("da_w", [128, 8, D], BF16)
                    cos = sb("da_cos", [128, TL], F32)
                    sin = sb("da_sin", [128, TL], F32)
                    rot = sb("da_rot", [128, 128], F32)
                    rotb = sb("da_rotb", [128, 128], BF16)
                    xbf = [sb("da_xbf%d" % i, [128, 512], BF16) for i in range(2)]
                    t1b = [sb("da_t1%d" % i, [128, 512], F32) for i in range(2)]
                    t2b = [sb("da_t2%d" % i, [128, 512], F32) for i in range(2)]
                    k.dma("sp", cos[:, :], self.const_d["ropecos"][:, :], [], ["cos"])
                    k.dma("sp", sin[:, :], self.const_d["ropesin"][:, :], [], ["sin"])
                    k.dma("sp", rot[:, :], self.const_d["roperot"][:, :], [], ["rot"])
                    k.copy("dve", rotb[:, :], rot[:, :], ["rot"], ["rotb"])
                    self.load_h(es, src, hT, 9)
                    hkeys = [("hT", b) for b in range(9)]
                    it = 0
                    self.tap("tap_hT", hT[:, 0, :], [128, T], hkeys + ["rotb", "cos", "sin"], BF16)
                    if getattr(self, "stop_after", "") == "loadh":
                        return
                    for part in range(2):
                        for kc in range(8):
                            k.dma("pool", w[:, kc, :], self.da_w_in[kc * 128:(kc + 1) * 128, part * D:(part + 1) * D],
                                  [], ["da_w"], append=(kc > 0))
                        dstT = qT if part == 0 else kT
                        dk = "qT" if part == 0 else "kT"
                        tblocks = [(i * 512, 512) for i in range(4)] + ([(TL, TC)] if part == 1 else [])
                        for h in range(8):
                            for (t0, n) in tblocks:
                                px = self.psb[it % 2]
                                kx = ("ps", it % 2)
                                for kc in range(8):
                                    k.mm(px[:, :n], w[:, kc, h * 128:(h + 1) * 128], hT[:, kc, t0:t0 + n], kc == 0, kc == 7,
                                         ["da_w"] + hkeys, [kx])
                                wkeys = [("qT", h, t0 // 256), ("qT", h, t0 // 256 + 1)] if part == 0 else [("kT", h)]
                                if t0 >= TL:
                                    k.copy("act", dstT[:, h, t0:t0 + n], px[:, :n], [kx], wkeys)
                                else:
                                    xb_ = xbf[it % 2]
                                    pr = self.psb[2 + it % 2]
                                    kr = ("ps", 2 + it % 2)
                                    k.copy("act", xb_[:, :n], px[:, :n], [kx], [("xbf", it % 2)])
                                    k.mm(pr[:, :n], rotb[:, :], xb_[:, :n], True, True, ["rotb", ("xbf", it % 2)], [kr])
                                    k.tt("dve", t1b[it % 2][:, :n], px[:, :n], cos[:, t0:t0 + n], ALU.mult, [kx, "cos"], [("t1", it % 2)])
                                    k.tt("dve", t2b[it % 2][:, :n], pr[:, :n], sin[:, t0:t0 + n], ALU.mult, [kr, "sin"], [("t2", it % 2)])
                                    k.tt("pool", dstT[:, h, t0:t0 + n], t1b[it % 2][:, :n], t2b[it % 2][:, :n], ALU.add,
                                         [("t1", it % 2), ("t2", it % 2)], wkeys)
                                it += 1
                    if getattr(self, "stop_after", "") == "qk":
                        self.tap("tap_qT", qT[:, 0, :], [128, TL], [("qT", 0, b_) for b_ in range(8)], BF16)
                        self.tap("tap_kT", kT[:, 0, :], [128, T], ("kT", 0), BF16)
                        return
                    for kc in range(8):
                        k.dma("pool", w[:, kc, :], self.da_w_in[kc * 128:(kc + 1) * 128, 2 * D:3 * D], [], ["da_w"], append=(kc > 0))
                    for tt_ in range(NKT):
                        for half in range(2):
                            pv = self.psb[4 + it % 2]
                            kv = ("ps", 4 + it % 2)
                            for kc in range(8):
                                k.mm(pv[:, :], hT[:, kc, tt_ * 128:(tt_ + 1) * 128], w[:, kc, half * 512:(half + 1) * 512],
                                     kc == 0, kc == 7, ["da_w"] + hkeys, [kv])
                            e = "act" if it % 2 == 0 else "dve"
                            k.copy(e, vt[:, tt_, half * 4:(half + 1) * 4, 0:128], pv[:, :].rearrange("p (h d) -> p h d", d=128),
                                   [kv], [("vt", tt_)])
                            it += 1
                self.tap("tap_qT", qT[:, 0, :], [128, TL], [("qT", 0, b_) for b_ in range(8)], BF16)
                self.tap("tap_kT", kT[:, 0, :], [128, T], ("kT", 0), BF16)
                self.tap("tap_vt", vt[:, :, 0, :], [128, NKT, 130], [("vt", t_) for t_ in range(NKT)] + ["vt_ones"], BF16)
                if getattr(self, "stop_after", "") == "proj":
                    return
                k.fence()
                with ExitStack() as es:
                    def sb(name, shape, dt):
                        return es.enter_context(self._sbt(name, list(shape), dt))
                    PT = [[sb("da_PT%d_%d" % (i, c), [128, NKT, QB], BF16) for c in range(2)] for i in range(2)]
                    small = sb("da_small", [128, 16], F32)
                    o1 = [sb("da_o1_%d" % i, [128, 128], F32) for i in range(2)]
                    o2 = [sb("da_o2_%d" % i, [128, 128], F32) for i in range(2)]
                    onb = [sb("da_onb_%d" % i, [128, 128], BF16) for i in range(2)]
                    sqj = sb("da_sqj", [128, 128], F32)
                    go_, _ = _vec_layout()["da_subln_g"]
                    gsc = sb("da_gsc", [128, 1], F32)
                    k.ts("dve", gsc[:, :], self.vecs[:, go_:go_ + 1], 1.0 - lam_init, None, ALU.mult, None, ["vecs"], ["gsc"])
                    vkeys = [("vt", t_) for t_ in range(NKT)] + ["vt_ones"]
                    mhalf = sb("da_mhalf", [128, 1], F32)
                    k.memset("pool", mhalf[:, :], -0.5, ["mhalf"])

                    def pv_post(h, qb, pset, itp, jt0):
                        q0 = qb * QB
                        for qt in range(QB // 128):
                            jt = jt0 + qt
                            pos = [self.psb[3 + 2 * (jt % 2)], self.psb[4 + 2 * (jt % 2)]]
                            kos = [("ps", 3 + 2 * (jt % 2)), ("ps", 4 + 2 * (jt % 2))]
                            for c in range(2):
                                for kt in range(NKT):
                                    k.mm(pos[c][:, 0:129], pset[c][:, kt, qt * 128:(qt + 1) * 128], vt[:, kt, h, 0:129],
                                         kt == 0, kt == NKT - 1, [("PT", itp, c)] + vkeys, [kos[c]])
                                    if kt % 2 == 1:
                                        yield
                            j = jt % 2
                            sm = small[:, j * 8:(j + 1) * 8]
                            smk = lambda i_, j=j: ("sm", j, i_)
                            k.recip(sm[:, 0:1], pos[0][:, 128:129], [kos[0]], [smk(0)])
                            k.recip(sm[:, 1:2], pos[1][:, 128:129], [kos[1]], [smk(1)])
                            yield
                            k.tt("dve", sm[:, 1:2], sm[:, 1:2], lam[:, 1:2], ALU.mult, [smk(1), "lam"], [smk(1)])
                            k.ts("dve", o2[j][:, :], pos[1][:, 0:128], sm[:, 1:2], None, ALU.mult, None, [kos[1], smk(1)], [("o2", j)])
                            yield
                            k.stt("dve", o1[j][:, :], pos[0][:, 0:128], sm[:, 0:1], o2[j][:, :], ALU.mult, ALU.add,
                                  [kos[0], smk(0), ("o2", j)], [("o1", j)])
                            yield
                            k._emit("act", lambda g, a_=sqj, b_=o1[j], c_=sm: g.activation(out=a_[:, :], in_=b_[:, :], func=AF.Square,
                                                                                     accum_out=c_[:, 2:3]),
                                    [("o1", j)], ["sqj", smk(2)])
                            yield
                            k.ts("dve", sm[:, 3:4], sm[:, 2:3], 1.0 / 128.0, RMS_EPS, ALU.mult, ALU.add, [smk(2)], [smk(3)])
                            k.tt("pool", sm[:, 3:4], sm[:, 3:4], mhalf[:, :], ALU.pow, [smk(3), "mhalf"], [smk(3)])
                            yield
                            k.ts("dve", onb[j][:, :], o1[j][:, :], sm[:, 3:4], None, ALU.mult, None, [("o1", j), smk(3)], [("onb", j)])
                            yield
                            ptr = self.psbf[1]
                            ktr = ("ps", 7)
                            k.tr(ptr[:, 0:128], onb[j][:, :], self.identb[:, :], [("onb", j), "identb"], [ktr])
                            yield
                            k.ts("dve", og[:, h, q0 + qt * 128:q0 + (qt + 1) * 128], ptr[:, 0:128], gsc[:, 0:1], None,
                                 ALU.mult, None, [ktr, "gsc"], [("qT", h, (q0 + qt * 128) // 256)])
                            yield

                    it = 0
                    jt = 0
                    pending = None
                    for h in range(8):
                        for qb in range(TL // QB):
                            q0 = qb * QB
                            pset = PT[it % 2]
                            for kt in range(NKT):
                                for c in range(2):
                                    psc = self.psb[(2 * kt + c) % 3]
                                    ksc = ("ps", (2 * kt + c) % 3)
                                    k.mm(psc[:, :QB], kT[c * 64:(c + 1) * 64, h, kt * 128:(kt + 1) * 128],
                                         qT[c * 64:(c + 1) * 64, h, q0:q0 + QB], True, True,
                                         [("kT", h)] + [("qT", h, (q0 + i_ * 256) // 256) for i_ in range(QB // 256)], [ksc])
                                    k.act(pset[c][:, kt, :], psc[:, :QB], AF.Exp, [ksc], [("PT", it % 2, c)], scale=0.125,
                                          bias=-16.0)
                                if pending is not None:
                                    for _ in range(4):
                                        if next(pending, "done") == "done":
                                            pending = None
                                            break
                            if pending is not None:
                                for _ in pending:
                                    pass
                            pending = pv_post(h, qb, pset, it % 2, jt)
                            jt += QB // 128
                            it += 1
                    for _ in pending:
                        pass
            self.tap("tap_og", og[:, 0, :], [128, TL], [("qT", 0, b_) for b_ in range(8)], BF16)
            if getattr(self, "stop_after", "") == "attn":
                return
            k.fence()
            with ExitStack() as es:
                self.mixer_out(es, layer, self.da_w_out, og, src, dst, 8, "xs1",
                               lambda h, bi: [("qT", h, bi)])


    def phase_dn(self, layer, src, dst):
        k = self.k
        nc = self.nc
        k.fence()
        NCK = T // 64
        TB = [(i * 512, 512) for i in range(4)] + [(TL, TC)]
        SEGS = [(0, TL), (TL, T)]
        dq = [k.dram("dn_%s" % nm, [8, 128, T], BF16) for nm in ("q", "k", "v", "z")]
        dn_og = k.dram("dn_og", [8, 128, T], BF16)
        co, _ = _vec_layout()["dn_conv"]
        with ExitStack() as es_g:
            def sbg(name, shape, dt):
                return es_g.enter_context(self._sbt(name, list(shape), dt))
            Gfm = [sbg("dn_G%d" % d_, [8, T], F32) for d_ in range(2)]
            Bfm = [sbg("dn_B%d" % d_, [8, T], F32) for d_ in range(2)]
            Gtm2 = sbg("dn_Gtm2", [128, NCK, 8], F32)
            Btm2 = sbg("dn_Btm2", [128, NCK, 8], F32)
            sel = sbg("dn_sel_sb", [8, 8, 128], F32)
            masks = sbg("dn_masks_sb", [128, 3, 64], F32)
            ident2 = sbg("dn_ident2_sb", [128, 64], F32)
            k.dma("sp", sel[:, :, :], self.const_d["dn_sel"].rearrange("k (r m) -> k r m", m=128), [], ["dn_sel"])
            k.dma("sp", masks[:, :, :], self.const_d["dn_masks2"].rearrange("p (a f) -> p a f", f=64), [], ["dn_masks"])
            k.dma("sp", ident2[:, :], self.const_d["dn_ident2"][:, :], [], ["dn_ident2"])
            with ExitStack() as es:
                def sb(name, shape, dt):
                    return es.enter_context(self._sbt(name, list(shape), dt))
                hT = sb("dn_hT", [128, 8, T], BF16)
                wbuf = [sb("dn_w%d" % i, [128, 8, D], BF16) for i in range(2)]
                wab = sb("dn_wab", [128, 8, 32], BF16)
                pre = [sb("dn_pre%d" % i, [128, T], BF16) for i in range(2)]
                dg = [sb("dn_dg%d" % i, [128, 5, 128], BF16) for i in range(2)]
                cv = [sb("dn_cv%d" % i, [128, T], F32) for i in range(2)]
                sqb = sb("dn_sqb", [128, T], BF16)
                rn = [sb("dn_rn%d" % i, [128, 512], F32) for i in range(2)]
                outb = [sb("dn_outb%d" % i, [128, T], BF16) for i in range(3)]
                gtmp = [sb("dn_gtmp%d" % i, [8, T], F32) for i in range(2)]
                nA = sb("dn_nA", [8, 2], F32)
                self.load_h(es, src, hT, 9)
                hkeys = [("hT", b) for b in range(9)]
                for kc in range(8):
                    k.dma("pool", wab[:, kc, :], self.dn_w_in[kc * 128:(kc + 1) * 128, 4096:4128], [], ["dn_wab"], append=(kc > 0))
                alo, _ = _vec_layout()["dn_a_log"]
                dbo, _ = _vec_layout()["dn_dt_bias"]
                k.act(nA[:, :], self.vecs[0:8, alo:alo + 2], AF.Exp, ["vecs"], ["dn_nA"])
                k.ts("dve", nA[:, :], nA[:, :], -1.0, None, ALU.mult, None, ["dn_nA"], ["dn_nA"])
                it = 0
                for d_ in range(2):
                    for (t0, n) in TB:
                        pa = self.psb[it % 2]
                        ka = ("ps", it % 2)
                        for kc in range(8):
                            k.mm(pa[0:8, :n], wab[:, kc, d_ * 8:d_ * 8 + 8], hT[:, kc, t0:t0 + n], kc == 0, kc == 7,
                                 ["dn_wab"] + hkeys, [ka])
                        k.act(Gfm[d_][:, t0:t0 + n], pa[0:8, :n], AF.Exp, [ka, "vecs"], [("Gfm", d_)],
                              bias=self.vecs[0:8, dbo + d_:dbo + d_ + 1])
                        it += 1
                        pb = self.psb[it % 2]
                        kb_ = ("ps", it % 2)
                        for kc in range(8):
                            k.mm(pb[0:8, :n], wab[:, kc, 16 + d_ * 8:16 + d_ * 8 + 8], hT[:, kc, t0:t0 + n], kc == 0, kc == 7,
                                 ["dn_wab"] + hkeys, [kb_])
                        k.act(Bfm[d_][:, t0:t0 + n], pb[0:8, :n], AF.Sigmoid, [kb_], [("Bfm", d_)])
                        it += 1
                    k.ts("dve", Gfm[d_][:, :], Gfm[d_][:, :], 1.0, None, ALU.add, None, [("Gfm", d_)], [("Gfm", d_)])
                    k.act(Gfm[d_][:, :], Gfm[d_][:, :], AF.Ln, [("Gfm", d_)], [("Gfm", d_)])
                    k.ts("dve", Gfm[d_][:, :], Gfm[d_][:, :], nA[:, d_:d_ + 1], None, ALU.mult, None, [("Gfm", d_), "dn_nA"], [("Gfm", d_)])
                    cur = Gfm[d_]
                    ck = ("Gfm", d_)
                    gi = 0
                    for sft in (1, 2, 4, 8, 16, 32):
                        nxt = gtmp[gi % 2]
                        nk = ("gtmp", gi % 2)
                        k.copy("pool", nxt[:, :], cur[:, :], [ck], [nk])
                        cv3 = cur[:, :].rearrange("p (c f) -> p c f", f=64)
                        nv3 = nxt[:, :].rearrange("p (c f) -> p c f", f=64)
                        if d_ == 0:
                            k.tt("dve", nv3[:, :, sft:], nv3[:, :, sft:], cv3[:, :, :64 - sft], ALU.add, [ck, nk], [nk])
                        else:
                            k.tt("dve", nv3[:, :, :64 - sft], nv3[:, :, :64 - sft], cv3[:, :, sft:], ALU.add, [ck, nk], [nk])
                        cur, ck = nxt, nk
                        gi += 1
                    k.copy("pool", Gfm[d_][:, :], cur[:, :], [ck], [("Gfm", d_)])
                    for (tile_, tmt, nm) in ((Gfm[d_], Gtm2, "Gtm2"), (Bfm[d_], Btm2, "Btm2")):
                        pt = self.psb[2 + it % 2]
                        kt_ = ("ps", 2 + it % 2)
                        hs = slice(d_ * 64, (d_ + 1) * 64)
                        for c in range(NCK):
                            k.mm(pt[hs, c * 8:(c + 1) * 8], tile_[0:8, c * 64:(c + 1) * 64], self.ident[0:8, 0:8], True, True,
                                 [("Gfm", d_), ("Bfm", d_), "ident"], [kt_])
                        k.copy("act", tmt[hs, :, :], pt[hs, 0:NCK * 8].rearrange("p (c r) -> p c r", r=8), [kt_], [(nm, d_)])
                        it += 1
                items = [(part, h) for part in range(4) for h in range(8)]
                NI = len(items)
                itc = [0]

                def load_w(part):
                    wt = wbuf[part % 2]
                    for kc in range(8):
                        k.dma("pool", wt[:, kc, :], self.dn_w_in[kc * 128:(kc + 1) * 128, part * D:(part + 1) * D],
                              [], [("dn_w", part % 2)], append=(kc > 0))

                def S1(i):
                    part, h = items[i]
                    if h == 0 and part + 1 < 4:
                        load_w(part + 1)
                    wt = wbuf[part % 2]
                    pj = i % 2
                    ob_ = outb[i % 3]
                    okey = ("dn_outb", i % 3)
                    for (t0, n) in TB:
                        px = self.psb[itc[0] % 2]
                        kx = ("ps", itc[0] % 2)
                        for kc in range(8):
                            k.mm(px[:, :n], wt[:, kc, h * 128:(h + 1) * 128], hT[:, kc, t0:t0 + n], kc == 0, kc == 7,
                                 [("dn_w", part % 2)] + hkeys, [kx])
                        if part == 3:
                            k.act(ob_[:, t0:t0 + n], px[:, :n], AF.Silu, [kx], [okey])
                        else:
                            k.copy("act", pre[pj][:, t0:t0 + n], px[:, :n], [kx], [("dn_pre", pj)])
                        itc[0] += 1

                def S2(i):
                    part, h = items[i]
                    if part == 3:
                        return
                    pj = i % 2
                    ob_ = outb[i % 3]
                    okey = ("dn_outb", i % 3)
                    ch = part * 8 + h
                    cvt = cv[pj]
                    ckey = ("dn_cv", pj)
                    dgt = dg[pj]
                    for j in range(5):
                        e = "pool" if j % 2 == 0 else "dve"
                        k.ts(e, dgt[:, j, :], self.identb[:, :], self.vecs[:, co + ch * 5 + j:co + ch * 5 + j + 1], None, ALU.mult, None,
                             ["identb", "vecs"], [("dn_dg", pj, j)])
                    for (t0, n) in TB:
                        s0, s1 = (0, TL) if t0 < TL else (TL, T)
                        pc = self.psb[4 + itc[0] % 2]
                        kc_ = ("ps", 4 + itc[0] % 2)
                        taps = [2, 0, 1, 3, 4]
                        for ti, j in enumerate(taps):
                            sh = j - 2
                            lo, hi = max(t0, s0 - sh), min(t0 + n, s1 - sh)
                            k.mm(pc[:, lo - t0:hi - t0], dgt[:, j, :], pre[pj][:, lo + sh:hi + sh], ti == 0, ti == 4,
                                 [("dn_dg", pj, j), ("dn_pre", pj)], [kc_])
                        if part == 2:
                            k.act(ob_[:, t0:t0 + n], pc[:, :n], AF.Silu, [kc_], [okey])
                        else:
                            k.act(cvt[:, t0:t0 + n], pc[:, :n], AF.Silu, [kc_], [ckey])
                        itc[0] += 1
                    if part == 0 and h == 0:
                        self.tap("tap_pre", pre[pj][:, :], [128, T], ("dn_pre", pj), BF16)
                        self.tap("tap_cv", cvt[:, :], [128, T], ckey)
                        self.tap("tap_hT0", hT[:, 0, :], [128, T], hkeys, BF16)

                def S3(i):
                    part, h = items[i]
                    pj = i % 2
                    ob_ = outb[i % 3]
                    okey = ("dn_outb", i % 3)
                    if part < 2:
                        cvt = cv[pj]
                        ckey = ("dn_cv", pj)
                        k.tt("pool", sqb[:, :], cvt[:, :], cvt[:, :], ALU.mult, [ckey], ["dn_sqb"])
                        for bi_, (t0, n) in enumerate(TB):
                            pn = self.psb[2 + itc[0] % 2]
                            kn = ("ps", 2 + itc[0] % 2)
                            k.mm(pn[:, :n], self.onesb[:, :], sqb[:, t0:t0 + n], True, True, ["onesb", "dn_sqb"], [kn])
                            r_ = rn[itc[0] % 2]
                            rk = ("dn_rn", itc[0] % 2)
                            k.ts("dve", r_[:, :n], pn[:, :n], RMS_EPS, None, ALU.add, None, [kn], [rk])
                            k.act(r_[:, :n], r_[:, :n], AF.Sqrt, [rk], [rk])
                            k.recip(r_[:, :n], r_[:, :n], [rk], [rk])
                            if part == 0:
                                k.stt("dve", ob_[:, t0:t0 + n], cvt[:, t0:t0 + n], 128.0 ** -0.5, r_[:, :n], ALU.mult, ALU.mult,
                                      [ckey, rk], [okey])
                            else:
                                k.tt("dve", ob_[:, t0:t0 + n], cvt[:, t0:t0 + n], r_[:, :n], ALU.mult, [ckey, rk], [okey])
                            itc[0] += 1
                    k.dma("sp", dq[part][h], ob_[:, :], [okey], [("dq", part, h)])

                load_w(0)
                for i in range(NI + 2):
                    if i < NI:
                        S1(i)
                    if 0 <= i - 1 < NI:
                        S2(i - 1)
                    if 0 <= i - 2 < NI:
                        S3(i - 2)
                self.tap("tap_Gfm0", Gfm[0][:, :], [8, T], ("Gfm", 0))
                self.tap("tap_Gfm1", Gfm[1][:, :], [8, T], ("Gfm", 1))
                self.tap("tap_Bfm0", Bfm[0][:, :], [8, T], ("Bfm", 0))
            if getattr(self, "stop_after", "") == "dnproj":
                self.dq = dq
                return
            k.fence()
            with ExitStack() as es:
                def sb(name, shape, dt):
                    return es.enter_context(self._sbt(name, list(shape), dt))
                qT = sb("dn_qT", [128, T], BF16)
                kT = sb("dn_kT", [128, T], BF16)
                vT = sb("dn_vT", [128, T], BF16)
                zs = qT
                ktm = sb("dn_ktm", [128, NCK, 128], BF16)
                vtm = sb("dn_vtm", [128, NCK, 128], BF16)
                oacc = [sb("dn_oacc%d" % d_, [128, T], F32) for d_ in range(2)]
                U = sb("dn_U", [128, NCK, 128], BF16)
                KD = sb("dn_KD", [128, NCK, 128], BF16)
                QK = sb("dn_QK", [128, T], BF16)
                WT = [sb("dn_WT%d" % d_, [128, T], BF16) for d_ in range(2)]
                QD = [sb("dn_QD%d" % d_, [128, T], BF16) for d_ in range(2)]
                DS = [sb("dn_DS%d" % d_, [128, NCK], F32) for d_ in range(2)]
                coef = sb("dn_coef", [128, NCK], F32)
                dtm = sb("dn_dtm", [128, NCK], F32)
                glast = sb("dn_glast", [128, NCK], F32)
                E = sb("dn_E", [128, 512], F32)
                F1 = sb("dn_F1", [128, 512], F32)
                gT_i = sb("dn_gTi", [128, 512], F32)
                gT_s = sb("dn_gTs", [128, 512], F32)
                g_s = sb("dn_gs", [128, 512], F32)
                kbT = [sb("dn_kbT%d" % d_, [128, 512], BF16) for d_ in range(2)]
                eG = sb("dn_eG", [128, 512], F32)
                PbS = [[sb("dn_P%d_%d" % (s_, i), [128, 512], F32) for i in range(2)] for s_ in range(2)]
                QbS = [[sb("dn_Q%d_%d" % (s_, i), [128, 512], F32) for i in range(2)] for s_ in range(2)]
                XfS = [sb("dn_Xf%d" % s_, [128, 512], F32) for s_ in range(2)]
                XbS = [sb("dn_Xb%d" % s_, [128, 512], BF16) for s_ in range(2)]
                vb = sb("dn_vb", [128, 8, 128], BF16)
                kg = sb("dn_kg", [128, 8, 128], BF16)
                S = [sb("dn_S%d" % d_, [128, 128], F32) for d_ in range(2)]
                Sb = [sb("dn_Sb%d" % d_, [128, 128], BF16) for d_ in range(2)]
                vn = sb("dn_vn", [128, 128], BF16)
                osum = eG
                osq = kbT[0]
                orn = sb("dn_orn", [128, 512], F32)
                ogt = vT
                ngo, _ = _vec_layout()["dn_norm_g"]
                HS = [slice(0, 64), slice(64, 128)]
                LASTF = [63, 0]
                pi = 0
                for h in range(8):
                    k.dma("sp", qT[:, :], dq[0][h], [("dq", 0, h)], ["dn_qT"])
                    k.dma("sp", kT[:, :], dq[1][h], [("dq", 1, h)], ["dn_kT"])
                    k.dma("sp", vT[:, :], dq[2][h], [("dq", 2, h)], ["dn_vT"])
                    for (srcT, dstm, sk, dk_) in ((kT, ktm, "dn_kT", "dn_ktm"), (vT, vtm, "dn_vT", "dn_vtm")):
                        for c0 in range(0, NCK, 8):
                            ncb = min(8, NCK - c0)
                            ptr = self.psbf[pi % 2]
                            ktr = ("ps", 6 + pi % 2)
                            for c in range(ncb):
                                for d_ in range(2):
                                    k.tr(ptr[HS[d_], c * 128:(c + 1) * 128], srcT[:, (c0 + c) * 64:(c0 + c + 1) * 64], self.identb[:, :],
                                         [sk, "identb"], [ktr])
                            e = "act" if pi % 2 == 0 else "dve"
                            k.copy(e, dstm[:, c0:c0 + ncb, :], ptr[:, 0:ncb * 128].rearrange("p (c d) -> p c d", d=128), [ktr], [dk_])
                            pi += 1
                    k.act(coef[:, :], Gtm2[:, :, h], AF.Exp, [("Gtm2", 0), ("Gtm2", 1)], ["dn_coef"])
                    k.tt("dve", coef[:, :], coef[:, :], Btm2[:, :, h], ALU.mult, ["dn_coef", ("Btm2", 0), ("Btm2", 1)], ["dn_coef"])
                    GK = [("Gtm2", 0), ("Gtm2", 1)]
                    BK = [("Btm2", 0), ("Btm2", 1)]

                    def early(t0, n, bp):
                        Pb, Qb, Xf = PbS[bp], QbS[bp], XfS[bp]
                        PK = lambda i_: ("dn_P", bp, i_)
                        QKY = lambda i_: ("dn_Q", bp, i_)
                        XK = ("dn_Xf", bp)
                        nch = n // 64
                        c0 = t0 // 64
                        pG = self.psb[0]
                        for d_ in range(2):
                            k.mm(pG[HS[d_], :n], sel[0:8, h, 0:64], Gfm[d_][0:8, t0:t0 + n], True, True, ["dn_sel", ("Gfm", d_)], [("ps", 0)])
                        pG3 = pG[:, :n].rearrange("p (c f) -> p c f", f=64)
                        k.tt("dve", E[:, :n].rearrange("p (c f) -> p c f", f=64), pG3,
                             Gtm2[:, c0:c0 + nch, h:h + 1].to_broadcast([128, nch, 64]), ALU.subtract,
                             [("ps", 0)] + GK, ["dn_E"])
                        for d_ in range(2):
                            k.copy("act", glast[HS[d_], c0:c0 + nch], pG[HS[d_], :n].rearrange("p (c f) -> p c f", f=64)[:, :, LASTF[d_]],
                                   [("ps", 0)], ["dn_glast"])
                        yield
                        for d_ in range(2):
                            pG2 = self.psb[1 + d_]
                            kG2 = ("ps", 1 + d_)
                            k.mm(pG2[:, :n], sel[0:8, h, :], Gfm[d_][0:8, t0:t0 + n], True, True, ["dn_sel", ("Gfm", d_)], [kG2])
                            k.act(eG[:, :n], pG2[:, :n], AF.Exp, [kG2], ["dn_eG"])
                            k.act(DS[d_][:, c0:c0 + nch], pG2[:, :n].rearrange("p (c f) -> p c f", f=64)[:, :, LASTF[d_]], AF.Exp,
                                  [kG2], [("dn_DS", d_)])
                            k.tt("dve", QD[d_][:, t0:t0 + n], qT[:, t0:t0 + n], eG[:, :n], ALU.mult, ["dn_qT", "dn_eG"], [("dn_QD", d_)])
                        for d_ in range(2):
                            pB = self.psb[6 + d_]
                            kB = ("ps", 6 + d_)
                            k.mm(pB[:, :n], sel[0:8, h, :], Bfm[d_][0:8, t0:t0 + n], True, True, ["dn_sel", ("Bfm", d_)], [kB])
                            k.tt("dve", kbT[d_][:, :n], kT[:, t0:t0 + n], pB[:, :n], ALU.mult, ["dn_kT", kB], [("dn_kbT", d_)])
                        yield
                        E3 = E[:, :n].rearrange("p (c f) -> p c f", f=64)

                        def mk(idx):
                            return masks[:, idx:idx + 1, :].to_broadcast([128, nch, 64])
                        k.stt("dve", F1[:, :n].rearrange("p (c f) -> p c f", f=64), E3, 0.0, mk(0), ALU.min, ALU.add,
                              ["dn_E", "dn_masks"], ["dn_F1"])
                        k.act(gT_i[:, :n], F1[:, :n], AF.Exp, ["dn_F1"], ["dn_gTi"])
                        yield
                        k.stt("dve", F1[:, :n].rearrange("p (c f) -> p c f", f=64), E3, 0.0, mk(1), ALU.min, ALU.add,
                              ["dn_E", "dn_masks", "dn_F1"], ["dn_F1"])
                        k.act(gT_s[:, :n], F1[:, :n], AF.Exp, ["dn_F1"], ["dn_gTs"])
                        k.stt("dve", F1[:, :n].rearrange("p (c f) -> p c f", f=64), E3, 0.0, mk(2), ALU.max, ALU.add,
                              ["dn_E", "dn_masks", "dn_F1"], ["dn_F1"])
                        k.act(g_s[:, :n], F1[:, :n], AF.Exp, ["dn_F1"], ["dn_gs"], scale=-1.0)
                        pN, pNT, pKQ = self.psb[0], self.psb[1], self.psb[2]
                        for c in range(nch):
                            cs = slice(t0 + c * 64, t0 + (c + 1) * 64)
                            bs = slice(c * 64, (c + 1) * 64)
                            for d_ in range(2):
                                k.mm(pN[HS[d_], bs], kT[:, cs], kbT[d_][:, bs], True, True, ["dn_kT", ("dn_kbT", d_)], [("ps", 0)])
                                k.mm(pNT[HS[d_], bs], kbT[d_][:, bs], kT[:, cs], True, True, ["dn_kT", ("dn_kbT", d_)], [("ps", 1)])
                                k.mm(pKQ[HS[d_], bs], kT[:, cs], qT[:, cs], True, True, ["dn_kT", "dn_qT"], [("ps", 2)])
                        yield
                        k.stt("dve", Xf[:, :n], pN[:, :n], -1.0, gT_s[:, :n], ALU.mult, ALU.mult, [("ps", 0), "dn_gTs"], [XK])
                        k.tt("dve", Qb[0][:, :n], pNT[:, :n], g_s[:, :n], ALU.mult, [("ps", 1), "dn_gs"], [QKY(0)])
                        k.tt("dve", QK[:, t0:t0 + n], pKQ[:, :n], gT_i[:, :n], ALU.mult, [("ps", 2), "dn_gTi"], ["dn_QK"])
                        k.act(Pb[0][:, :n], Xf[:, :n], AF.Copy, [XK], [PK(0)], scale=-1.0)
                        k.tt("pool", Xf[:, :n].rearrange("p (c f) -> p c f", f=64), Xf[:, :n].rearrange("p (c f) -> p c f", f=64),
                             ident2[:, :].unsqueeze(1).to_broadcast([128, nch, 64]), ALU.add, [XK, "dn_ident2"], [XK])
                        yield

                    def doubling(t0, n, bp):
                        Pb, Qb, Xf, Xb = PbS[bp], QbS[bp], XfS[bp], XbS[bp]
                        PK = lambda i_: ("dn_P", bp, i_)
                        QKY = lambda i_: ("dn_Q", bp, i_)
                        XK = ("dn_Xf", bp)
                        nch = n // 64
                        pP, pQ, pX = self.psb[3], self.psb[4], self.psb[5]
                        for st_ in range(1, 6):
                            a_, b_ = (st_ - 1) % 2, st_ % 2
                            if st_ < 5:
                                for c in range(nch):
                                    bs = slice(c * 64, (c + 1) * 64)
                                    for d_ in range(2):
                                        k.mm(pP[HS[d_], bs], Qb[a_][HS[d_], bs], Pb[a_][HS[d_], bs], True, True,
                                             [PK(a_), QKY(a_)], [("ps", 3)])
                            for c in range(nch):
                                bs = slice(c * 64, (c + 1) * 64)
                                for d_ in range(2):
                                    k.mm(pQ[HS[d_], bs], Pb[a_][HS[d_], bs], Qb[a_][HS[d_], bs], True, True,
                                         [PK(a_), QKY(a_)], [("ps", 4)])
                            yield
                            if st_ < 5:
                                k.copy("act", Pb[b_][:, :n], pP[:, :n], [("ps", 3)], [PK(b_)])
                            k.copy("dve", Qb[b_][:, :n], pQ[:, :n], [("ps", 4)], [QKY(b_)])
                            yield
                            for c in range(nch):
                                bs = slice(c * 64, (c + 1) * 64)
                                for d_ in range(2):
                                    k.mm(pX[HS[d_], bs], Qb[b_][HS[d_], bs], Xf[HS[d_], bs], True, True,
                                         [QKY(b_), XK], [("ps", 5)])
                            yield
                            k.tt("dve", Xf[:, :n], Xf[:, :n], pX[:, :n], ALU.add, [XK, ("ps", 5)], [XK])
                            yield
                        k.copy("act", Xb[:, :n], Xf[:, :n], [XK], [("dn_Xb", bp)])

                    def late(t0, n, bp):
                        Xb = XbS[bp]
                        XBK = ("dn_Xb", bp)
                        nch = n // 64
                        c0 = t0 // 64
                        k.tt("pool", vb[:, :nch, :], vtm[:, c0:c0 + nch, :], Btm2[:, c0:c0 + nch, h:h + 1].to_broadcast([128, nch, 128]),
                             ALU.mult, ["dn_vtm"] + BK, ["dn_vb"])
                        k.tt("pool", kg[:, :nch, :], ktm[:, c0:c0 + nch, :], coef[:, c0:c0 + nch].unsqueeze(2).to_broadcast([128, nch, 128]),
                             ALU.mult, ["dn_ktm", "dn_coef"], ["dn_kg"])
                        k.tt("dve", dtm[:, c0:c0 + nch], glast[:, c0:c0 + nch], Gtm2[:, c0:c0 + nch, h], ALU.subtract,
                             ["dn_glast"] + GK, ["dn_dtm"])
                        k.act(dtm[:, c0:c0 + nch], dtm[:, c0:c0 + nch], AF.Exp, ["dn_dtm"], ["dn_dtm"])
                        k.tt("pool", KD[:, c0:c0 + nch, :], ktm[:, c0:c0 + nch, :],
                             dtm[:, c0:c0 + nch].unsqueeze(2).to_broadcast([128, nch, 128]), ALU.mult, ["dn_ktm", "dn_dtm"], ["dn_KD"])
                        for half in range(0, nch, 4):
                            pU = self.psb[6 + half // 4 % 2]
                            kU = ("ps", 6 + half // 4 % 2)
                            nh = min(4, nch - half)
                            for c in range(nh):
                                cc = half + c
                                for d_ in range(2):
                                    k.mm(pU[HS[d_], c * 128:(c + 1) * 128], Xb[HS[d_], cc * 64:(cc + 1) * 64], vb[HS[d_], cc, :], True, True,
                                         [XBK, "dn_vb"], [kU])
                            k.copy("act", U[:, c0 + half:c0 + half + nh, :], pU[:, 0:nh * 128].rearrange("p (c d) -> p c d", d=128),
                                   [kU], ["dn_U"])
                        for d_ in range(2):
                            pW = self.psb[1 + d_]
                            kW = ("ps", 1 + d_)
                            for c in range(nch):
                                k.mm(pW[:, c * 64:(c + 1) * 64], kg[HS[d_], c, :], Xb[HS[d_], c * 64:(c + 1) * 64], True, True, ["dn_kg", XBK], [kW])
                            k.copy("act" if d_ == 0 else "dve", WT[d_][:, t0:t0 + n], pW[:, :n], [kW], [("dn_WT", d_)])

                    def run(g):
                        for _ in g:
                            pass
                    run(early(TB[0][0], TB[0][1], 0))
                    for bi_ in range(len(TB)):
                        t0, n = TB[bi_]
                        dg_ = doubling(t0, n, bi_ % 2)
                        if bi_ + 1 < len(TB):
                            eg_ = early(TB[bi_ + 1][0], TB[bi_ + 1][1], (bi_ + 1) % 2)
                            live = [dg_, eg_]
                            while live:
                                for g_ in list(live):
                                    if next(g_, "done") == "done":
                                        live.remove(g_)
                        else:
                            run(dg_)
                        late(t0, n, bi_ % 2)
                    if getattr(self, "stop_after", "") == "dnpre":
                        self.tap("tap_U", U[:, :, :], [128, NCK, 128], "dn_U", BF16)
                        self.tap("tap_WT0", WT[0][:, :], [128, T], ("dn_WT", 0), BF16)
                        self.tap("tap_WT1", WT[1][:, :], [128, T], ("dn_WT", 1), BF16)
                        self.tap("tap_Xf", XfS[0][:, :], [128, 512], ("dn_Xf", 0))
                        return
                    for d_ in range(2):
                        k.memset("pool", S[d_][:, :], 0.0, [("dn_S", d_)])
                        k.memset("pool", Sb[d_][:, :], 0.0, [("dn_Sb", d_)])
                    order = [list(range(32, 36)) + list(range(0, 32)), list(range(35, 31, -1)) + list(range(31, -1, -1))]
                    for step in range(NCK):
                        for d_ in range(2):
                            c = order[d_][step]
                            cs = slice(c * 64, (c + 1) * 64)
                            hs = HS[d_]
                            p1, p2, p3 = self.psb[d_ * 3], self.psb[d_ * 3 + 1], self.psb[d_ * 3 + 2]
                            k1, k2, k3 = ("ps", d_ * 3), ("ps", d_ * 3 + 1), ("ps", d_ * 3 + 2)
                            vk = ("dn_vn", d_)
                            k.mm(p1[hs, 0:128], WT[d_][:, cs], Sb[d_][:, :], True, True, [("dn_WT", d_), ("dn_Sb", d_)], [k1])
                            k.tt("dve", vn[hs, :], U[hs, c, :], p1[hs, 0:128], ALU.subtract, ["dn_U", k1], [vk])
                            k.mm(p2[:, 0:64], Sb[d_][:, :], QD[d_][:, cs], True, False, [("dn_Sb", d_), ("dn_QD", d_)], [k2])
                            k.mm(p2[:, 0:64], vn[hs, :], QK[hs, cs], False, True, [vk, "dn_QK"], [k2])
                            k.copy("act", oacc[d_][:, cs], p2[:, 0:64], [k2], [("dn_oacc", d_)])
                            k.mm(p3[:, 0:128], KD[hs, c, :], vn[hs, :], True, True, ["dn_KD", vk], [k3])
                            k.stt("dve", Sb[d_][:, :], S[d_][:, :], DS[d_][:, c:c + 1], p3[:, 0:128], ALU.mult, ALU.add,
                                  [("dn_S", d_), ("dn_DS", d_), k3], [("dn_Sb", d_)])
                            k.stt("dve", S[d_][:, :], S[d_][:, :], DS[d_][:, c:c + 1], p3[:, 0:128], ALU.mult, ALU.add,
                                  [("dn_S", d_), ("dn_DS", d_), k3], [("dn_S", d_)])
                    k.dma("sp", zs[:, :], dq[3][h], [("dq", 3, h)], ["dn_qT"])
                    for (t0, n) in TB:
                        k.tt("dve", osum[:, :n], oacc[0][:, t0:t0 + n], oacc[1][:, t0:t0 + n], ALU.add, [("dn_oacc", 0), ("dn_oacc", 1)], ["dn_eG"])
                        k.tt("pool", osq[:, :n], osum[:, :n], osum[:, :n], ALU.mult, ["dn_eG"], [("dn_kbT", 0)])
                        pn = self.psb[6]
                        k.mm(pn[:, :n], self.onesb[:, :], osq[:, :n], True, True, ["onesb", ("dn_kbT", 0)], [("ps", 6)])
                        k.ts("dve", orn[:, :n], pn[:, :n], 1.0 / 128.0, RMS_EPS, ALU.mult, ALU.add, [("ps", 6)], ["dn_orn"])
                        k.act(orn[:, :n], orn[:, :n], AF.Sqrt, ["dn_orn"], ["dn_orn"])
                        k.recip(orn[:, :n], orn[:, :n], ["dn_orn"], ["dn_orn"])
                        k.stt("dve", osum[:, :n], osum[:, :n], self.vecs[:, ngo:ngo + 1], orn[:, :n], ALU.mult, ALU.mult,
                              ["dn_eG", "vecs", "dn_orn"], ["dn_eG"])
                        k.tt("pool", ogt[:, t0:t0 + n], osum[:, :n], zs[:, t0:t0 + n], ALU.mult, ["dn_eG", "dn_qT"], ["dn_vT"])
                    k.dma("sp", dn_og[h], ogt[:, :], ["dn_vT"], ["dn_og"], append=(h > 0))
                    if getattr(self, "stop_after", "") == "dnhead0":
                        self.tap("tap_ogt", ogt[:, :], [128, T], "dn_vT", BF16)
                        return
        k.fence()
        with ExitStack() as es:
            self.mixer_out(es, layer, self.dn_w_out, None, src, dst, 9, "xs1", None, og_dram=dn_og)


def prep_inputs(inp, b):
    consts = make_consts()
    xin = np.ascontiguousarray(np.concatenate([inp["x"][b].T, inp["ctx"][b].T], axis=1))
    cv = np.stack([inp["c"][b].reshape(8, 128).T, inp["c_ctx"].reshape(8, 128).T], axis=2).reshape(128, 16)
    m = {
        "xin": xin.astype(np.float32),
        "cvec": np.ascontiguousarray(cv, dtype=np.float32),
        "vecs": pack_vecs(inp),
    }
    for nm in CONST_SHAPES:
        m[nm] = consts[nm]
    for nm in ("ada_w", "ffn_w_gate", "ffn_w_up", "ffn_w_down"):
        m[nm] = np.ascontiguousarray(inp[nm], dtype=np.float32)
    for nm in ("da_w_in", "da_w_out", "dn_w_in", "dn_w_out"):
        m[nm] = np.ascontiguousarray(inp[nm][0], dtype=np.float32)
    return m


def kernel(**inputs):
    inp = {k_: np.asarray(v) for k_, v in inputs.items()}
    p = Prog([("ada", 0), ("dn", 0), ("ffn", 0), ("ada", 1), ("da", 1), ("ffn", 1)])
    nc = p.build()
    in_maps = [prep_inputs(inp, b) for b in range(8)]
    res = run_bass_kernel_spmd(nc, in_maps, core_ids=list(range(8)))
    out = np.stack([np.ascontiguousarray(r["outT"].T) for r in res.results], axis=0)
    return out.astype(np.float32)
```

```python
import math
from contextlib import ExitStack

import numpy as np
import concourse.bass as bass
import concourse.mybir as mybir
from concourse.bass_utils import run_bass_kernel_spmd

F32 = mybir.dt.float32
BF16 = mybir.dt.bfloat16
AF = mybir.ActivationFunctionType
ALU = mybir.AluOpType

D = 1024
NCH = 8
TL = 2048
TC = 256
T = TL + TC
FF = 2816
NFF = 22
ALPHA = (2.0 * 2) ** 0.25
LN_EPS = 1e-5
RMS_EPS = 1e-6
NB = 256
BLOCKS = [(i * NB, NB) for i in range(T // NB)]


def blk_set(t0):
    return 0 if t0 < TL else 1


class K:
    def __init__(self, nc, es):
        self.nc = nc
        self.es = es
        self.engs = {"pe": nc.tensor, "act": nc.scalar, "dve": nc.vector,
                     "pool": nc.gpsimd, "sp": nc.sync}
        self.esem = {}
        for e in self.engs:
            self.esem[e] = es.enter_context(nc.semaphore("sem_" + e))
        self.ecnt = {e: 0 for e in self.engs}
        self.seen = {e: {} for e in self.engs}
        self.sems = {}
        for e in self.engs:
            self.sems["sem_" + e] = self.esem[e]
        self.lastw = {}
        self.rd = {}
        self.dsem = {}
        self.ninst = 0
        self.nwait = 0
        self.fence_ev = {}

    def fence(self):
        ev = {}
        for e in self.engs:
            if self.ecnt[e]:
                ev["sem_" + e] = self.ecnt[e]
        for nm, cnt in self.dsem.values():
            if cnt:
                ev[nm] = cnt
        self.fence_ev = ev

    def sb(self, name, shape, dt):
        return self.es.enter_context(self.nc.sbuf_tensor(name, list(shape), dt))

    def ps(self, name, shape, dt=F32):
        return self.es.enter_context(self.nc.psum_tensor(name, list(shape), dt))

    def dram(self, name, shape, dt, kind="Internal"):
        return self.nc.dram_tensor(name, list(shape), dt, kind=kind).ap()

    def _emit(self, e, fn, rd, wr, dma=False, append=False, noinc=False):
        need = {}

        def add(ev):
            if ev is None:
                return
            s, v = ev
            if e == "pe" and s == "sem_pe":
                return
            if need.get(s, 0) < v:
                need[s] = v

        for r in rd:
            add(self.lastw.get(r))
            if isinstance(r, tuple) and r[0] == "ps":
                for s_, v_ in self.rd.get(r, {}).items():
                    if s_ != "sem_" + e:
                        add((s_, v_))
        if wr:
            for s_, v_ in self.fence_ev.items():
                if not (s_ == "sem_" + e and e != "pe"):
                    add((s_, v_)) if not (e == "pe" and s_ == "sem_pe") else None
                else:
                    add((s_, v_))
        for w in wr:
            lw = self.lastw.get(w)
            if not (append and dma and lw is not None and w in self.dsem and lw[0] == self.dsem[w][0]):
                add(lw)
            for s, v in self.rd.get(w, {}).items():
                add((s, v))
        eng = self.engs[e]
        seen = self.seen[e]
        todo = [(s, v) for s, v in need.items() if seen.get(s, 0) < v]
        attach = None
        if todo and not dma:
            attach = todo.pop()
        for s, v in todo:
            eng.wait_ge(self.sems[s], v)
            seen[s] = v
            self.nwait += 1
        ins = fn(eng)
        if attach is not None:
            ins._wait_ge(self.sems[attach[0]], attach[1])
            seen[attach[0]] = attach[1]
        self.ninst += 1
        if dma:
            w0 = wr[0]
            if w0 not in self.dsem:
                nm = "dsem%d" % len(self.dsem)
                self.sems[nm] = self.es.enter_context(self.nc.semaphore(nm))
                self.dsem[w0] = [nm, 0]
            ds = self.dsem[w0]
            ds[1] += 16
            ins.then_inc(self.sems[ds[0]], 16)
            ev = (ds[0], ds[1])
        elif noinc:
            assert e == "pe"
            ev = ("sem_" + e, self.ecnt[e] + 1)
        else:
            self.ecnt[e] += 1
            ins.then_inc(self.esem[e], 1)
            ev = ("sem_" + e, self.ecnt[e])
        for r in rd:
            d = self.rd.setdefault(r, {})
            if d.get(ev[0], 0) < ev[1]:
                d[ev[0]] = ev[1]
        for w in wr:
            self.lastw[w] = ev
            self.rd[w] = {}
        return ev

    def wait_all(self, e, keys):
        eng = self.engs[e]
        for k in keys:
            ev = self.lastw.get(k)
            if ev is None:
                continue
            eng.wait_ge(self.sems[ev[0]], ev[1])

    def dma(self, q, out, in_, rd, wr, append=False):
        return self._emit(q, lambda g: g.dma_start(out=out, in_=in_), rd, wr, dma=True, append=append)

    def mm(self, out, lhsT, rhs, start, stop, rd, wr):
        return self._emit("pe", lambda g: g.matmul(out, lhsT, rhs, start=start, stop=stop), rd, wr, noinc=not stop)

    def tr(self, out, in_, ident, rd, wr):
        return self._emit("pe", lambda g: g.transpose(out, in_, ident), rd, wr)

    def act(self, out, in_, func, rd, wr, bias=None, scale=None, e="act"):
        kw = {}
        if bias is not None:
            kw["bias"] = bias
        if scale is not None:
            kw["scale"] = scale
        return self._emit(e, lambda g: g.activation(out=out, in_=in_, func=func, **kw), rd, wr)

    def ts(self, e, out, in0, s1, s2, op0, op1, rd, wr):
        if s2 is None:
            return self._emit(e, lambda g: g.tensor_scalar(out=out, in0=in0, scalar1=s1, scalar2=None, op0=op0), rd, wr)
        return self._emit(e, lambda g: g.tensor_scalar(out=out, in0=in0, scalar1=s1, scalar2=s2, op0=op0, op1=op1), rd, wr)

    def tt(self, e, out, in0, in1, op, rd, wr):
        return self._emit(e, lambda g: g.tensor_tensor(out=out, in0=in0, in1=in1, op=op), rd, wr)

    def stt(self, e, out, in0, scalar, in1, op0, op1, rd, wr):
        return self._emit(e, lambda g: g.scalar_tensor_tensor(out=out, in0=in0, scalar=scalar, in1=in1, op0=op0, op1=op1), rd, wr)

    def copy(self, e, out, in_, rd, wr):
        if e == "act":
            return self._emit(e, lambda g: g.activation(out=out, in_=in_, func=AF.Copy), rd, wr)
        return self._emit(e, lambda g: g.tensor_copy(out=out, in_=in_), rd, wr)

    def memset(self, e, ap, val, wr):
        return self._emit(e, lambda g: g.memset(ap, val), [], wr)

    def recip(self, out, in_, rd, wr):
        return self._emit("dve", lambda g: g.reciprocal(out=out, in_=in_), rd, wr)


VEC_LAYOUT = {}


def _vec_layout():
    if VEC_LAYOUT:
        return VEC_LAYOUT
    off = 0

    def add(name, n):
        nonlocal off
        VEC_LAYOUT[name] = (off, n)
        off += n

    add("ada_b", 2 * 48)
    for nm in ("ln1_g", "ln1_b", "ln2_g", "ln2_b"):
        add(nm, 2 * 8)
    add("dn_conv", 24 * 5)
    add("dn_norm_g", 1)
    add("da_subln_g", 1)
    add("dn_a_log", 2)
    add("dn_dt_bias", 2)
    add("da_lambda", 4)
    VEC_LAYOUT["_total"] = (off, 0)
    return VEC_LAYOUT


def pack_vecs(inp):
    L = _vec_layout()
    tot = L["_total"][0]
    v = np.zeros((128, tot), np.float32)

    def put(name, arr):
        o, n = L[name]
        assert arr.shape == (arr.shape[0], n), (name, arr.shape, n)
        v[: arr.shape[0], o:o + n] = arr

    put("ada_b", np.concatenate([inp["ada_b"][i].reshape(48, 128).T for i in range(2)], axis=1))
    for nm in ("ln1_g", "ln1_b", "ln2_g", "ln2_b"):
        put(nm, np.concatenate([inp[nm][i].reshape(8, 128).T for i in range(2)], axis=1))
    cv = inp["dn_conv"][0]
    put("dn_conv", cv.reshape(5, 24, 128).transpose(2, 1, 0).reshape(128, 120))
    put("dn_norm_g", inp["dn_norm_g"][0].reshape(128, 1))
    put("da_subln_g", inp["da_subln_g"][0].reshape(128, 1))
    put("dn_a_log", inp["dn_a_log"][0].T.copy())
    put("dn_dt_bias", inp["dn_dt_bias"][0].T.copy())
    put("da_lambda", inp["da_lambda"][0].T.copy())
    return v


def make_consts():
    c = {}
    c["ident"] = np.eye(128, dtype=np.float32)
    c["onesD"] = np.full((128, 128), 1.0 / D, np.float32)
    c["ones"] = np.ones((128, 128), np.float32)
    quarter = 16
    inv_freq = 10000.0 ** (-np.arange(quarter, dtype=np.float32) / quarter)
    rows = TL // 64
    row = np.repeat(np.arange(rows, dtype=np.float32), 64)
    col = np.tile(np.arange(64, dtype=np.float32), rows)
    ang_r = row[:, None] * inv_freq
    ang_c = col[:, None] * inv_freq
    ang = np.concatenate([ang_r, ang_r, ang_c, ang_c], axis=-1)
    c["ropecos"] = np.ascontiguousarray(np.concatenate([np.cos(ang).T, np.cos(ang).T], axis=0), dtype=np.float32)
    c["ropesin"] = np.ascontiguousarray(np.concatenate([np.sin(ang).T, np.sin(ang).T], axis=0), dtype=np.float32)
    R = np.zeros((128, 128), np.float32)
    for comp in range(2):
        for half in range(2):
            b0 = comp * 64 + half * 32
            for d in range(16):
                R[b0 + 16 + d, b0 + d] = -1.0
                R[b0 + d, b0 + 16 + d] = 1.0
    c["roperot"] = R
    sel = np.zeros((8, 8, 128), np.float32)
    for r in range(8):
        sel[r, r, :] = 1.0
    c["dn_sel"] = sel.reshape(8, 8 * 128)
    p = np.arange(64)[:, None]
    f = np.arange(64)[None, :]
    BIG = 30000.0
    masks = np.stack([
        np.where(f >= p, 0.0, -BIG),
        np.where(f > p, 0.0, -BIG),
        np.where(f < p, 0.0, BIG),
        np.where(f <= p, 0.0, -BIG),
        np.where(f < p, 0.0, -BIG),
        np.where(f > p, 0.0, BIG),
    ], axis=1).astype(np.float32)
    c["dn_masks"] = np.ascontiguousarray(masks.reshape(64, 6 * 64))
    c["dn_masks2"] = np.ascontiguousarray(np.concatenate([masks[:, 0:3, :], masks[:, 3:6, :]], axis=0).reshape(128, 3 * 64))
    c["dn_ident2"] = np.ascontiguousarray(np.concatenate([np.eye(64), np.eye(64)], axis=0).astype(np.float32))
    return c


CONST_SHAPES = {"ident": [128, 128], "onesD": [128, 128], "ones": [128, 128],
                "ropecos": [128, TL], "ropesin": [128, TL], "roperot": [128, 128],
                "dn_sel": [8, 8 * 128], "dn_masks": [64, 6 * 64],
                "dn_masks2": [128, 3 * 64], "dn_ident2": [128, 64]}


class Prog:
    def __init__(self, phases, ext=()):
        self.phases = phases
        self.ext = ext
        self.nc = bass.Bass("TRN2", target_bir_lowering=False)
        self.es = ExitStack()
        self.k = K(self.nc, self.es)
        self.tapkeys = []

    def _sbt(self, name, shape, dt):
        self._uid = getattr(self, "_uid", 0) + 1
        return self.nc.sbuf_tensor("%s_u%d" % (name, self._uid), shape, dt)

    def build(self):
        nc, k = self.nc, self.k
        L = _vec_layout()
        NV = L["_total"][0]
        self.xin = k.dram("xin", [D, T], F32, kind="ExternalInput")
        self.cvec_d = k.dram("cvec", [128, 16], F32, kind="ExternalInput")
        self.vecs_d = k.dram("vecs", [128, NV], F32, kind="ExternalInput")
        self.const_d = {nm: k.dram(nm, shp, F32, kind="ExternalInput") for nm, shp in CONST_SHAPES.items()}
        self.ada_w = k.dram("ada_w", [2, D, 6 * D], F32, kind="ExternalInput")
        self.w_gate = k.dram("ffn_w_gate", [2, D, FF], F32, kind="ExternalInput")
        self.w_up = k.dram("ffn_w_up", [2, D, FF], F32, kind="ExternalInput")
        self.w_down = k.dram("ffn_w_down", [2, FF, D], F32, kind="ExternalInput")
        self.da_w_in = k.dram("da_w_in", [D, 3 * D], F32, kind="ExternalInput")
        self.da_w_out = k.dram("da_w_out", [D, D], F32, kind="ExternalInput")
        self.dn_w_in = k.dram("dn_w_in", [D, 4128], F32, kind="ExternalInput")
        self.dn_w_out = k.dram("dn_w_out", [D, D], F32, kind="ExternalInput")
        self.outT = k.dram("outT", [D, TL], F32, kind="ExternalOutput")
        self.xs = [k.dram(nm, [D, T], F32, kind=("ExternalOutput" if nm in self.ext else "Internal"))
                   for nm in ("xs0", "xs1")]

        self.vecs = k.sb("vecs_sb", [128, NV], F32)
        self.ident = k.sb("ident_sb", [128, 128], F32)
        self.identb = k.sb("identb_sb", [128, 128], BF16)
        self.onesD = k.sb("onesD_sb", [128, 128], F32)
        self.ones = k.sb("ones_sb", [128, 128], F32)
        self.onesb = k.sb("onesb_sb", [128, 128], BF16)
        self.svec = k.sb("svec", [128, 16], F32)
        self.mod = k.sb("mod", [128, 96], F32)
        self.modp = k.sb("modp", [128, 96], F32)
        self.psb = [k.ps("psb%d" % i, [128, 512], F32) for i in range(8)]
        self.psbf = [self.psb[6][:, :].bitcast(BF16), self.psb[7][:, :].bitcast(BF16)]

        k.dma("sp", self.vecs[:, :], self.vecs_d[:, :], [], ["vecs"])
        k.dma("sp", self.ident[:, :], self.const_d["ident"][:, :], [], ["ident"])
        k.dma("sp", self.onesD[:, :], self.const_d["onesD"][:, :], [], ["onesD"])
        k.dma("sp", self.ones[:, :], self.const_d["ones"][:, :], [], ["ones"])
        k.dma("sp", self.svec[:, :], self.cvec_d[:, :], [], ["svec"])
        k.act(self.svec[:, :], self.svec[:, :], AF.Silu, ["svec"], ["svec"])
        k.copy("dve", self.identb[:, :], self.ident[:, :], ["ident"], ["identb"])
        k.copy("dve", self.onesb[:, :], self.ones[:, :], ["ones"], ["onesb"])

        cur = self.xin
        for ph in self.phases:
            kind, layer = ph
            if kind == "ada":
                self.phase_ada(layer)
            elif kind == "ffn":
                last = (layer == 1)
                dst = self.outT if last else self.xs[0]
                self.phase_ffn(layer, cur, dst, last)
                cur = dst
            elif kind == "da":
                self.phase_da(layer, cur, self.xs[1])
                cur = self.xs[1]
            elif kind == "dn":
                self.phase_dn(layer, cur, self.xs[1])
                cur = self.xs[1]
        keys = list(self.tapkeys)
        for nm in ("out", "xs0", "xs1"):
            keys += [(nm, b) for b in range(len(BLOCKS))]
        k.wait_all("sp", keys)
        return nc

    def tap(self, name, ap, shape, key, dt=F32):
        if name not in self.ext:
            return
        d = self.k.dram(name, list(shape), dt, kind="ExternalOutput")
        self.k.dma("sp", d, ap, [key] if not isinstance(key, list) else key, [("tap", name)])
        self.tapkeys.append(("tap", name))

    def vcol(self, name, idx):
        o, n = _vec_layout()[name]
        return self.vecs[:, o + idx:o + idx + 1]

    def modcol(self, m, c, s, plus1=False):
        j = m * 8 + c
        t = self.modp if plus1 else self.mod
        return t[:, j * 2 + s:j * 2 + s + 1]

    def phase_ada(self, layer):
        k = self.k
        k.fence()
        GW = 768
        NG = 6 * D // GW
        with ExitStack() as es:
            wbuf = [es.enter_context(self._sbt("adaw%d" % i, [128, 8, GW], F32)) for i in range(2)]
            ps = self.psb[0]
            src = self.ada_w[layer].rearrange("(kc p) n -> p kc n", p=128)
            for g in range(NG):
                wb = wbuf[g % 2]
                keys = [("adaw", g % 2, h) for h in range(2)]
                k.dma("sp", wb[:, 0:4, :], src[:, 0:4, g * GW:(g + 1) * GW], [], [keys[0]])
                k.dma("act", wb[:, 4:8, :], src[:, 4:8, g * GW:(g + 1) * GW], [], [keys[1]])
                for jj in range(GW // 128):
                    j = g * (GW // 128) + jj
                    for kc in range(8):
                        k.mm(ps[:, j * 2:j * 2 + 2], wb[:, kc, jj * 128:(jj + 1) * 128],
                             self.svec[:, kc * 2:kc * 2 + 2], kc == 0, kc == 7,
                             [keys[kc // 4], "svec"], [("ps", 0)])
            o, n = _vec_layout()["ada_b"]
            bcol = self.vecs[:, o + layer * 48:o + layer * 48 + 48]
            k.tt("dve", self.mod[:, :].rearrange("p (j s) -> p j s", s=2),
                 ps[:, 0:96].rearrange("p (j s) -> p j s", s=2),
                 bcol.unsqueeze(2).to_broadcast([128, 48, 2]), ALU.add,
                 [("ps", 0), "vecs"], ["mod"])
            k.ts("dve", self.modp[:, :], self.mod[:, :], 1.0, None, ALU.add, None, ["mod"], ["modp"])
            self.tap("tap_mod%d" % layer, self.mod[:, :], [128, 96], "mod")

    def ln_block(self, r, n, gname, bname, layer, outb, rkey, okey, tmp, extra_wr=()):
        k = self.k
        sq, mean, var = tmp["sq"], tmp["mean"], tmp["var"]
        inplace = outb is None
        if inplace:
            outb = sq
        psm, psv = self.psb[6], self.psb[7]
        for c in range(8):
            k.act(sq[:, c, :n], r[:, c, :n], AF.Square, [rkey], [("ln_sq", c)] + (list(extra_wr) if c == 0 else []))
        for c in range(8):
            k.mm(psm[:, :n], self.onesD[:, :], r[:, c, :n], c == 0, c == 7, ["onesD", rkey], [("ps", 6)])
        for c in range(8):
            k.mm(psv[:, :n], self.onesD[:, :], sq[:, c, :n], c == 0, c == 7, ["onesD", ("ln_sq", c)], [("ps", 7)])
        k.copy("act", mean[:, :n], psm[:, :n], [("ps", 6)], ["ln_mean"])
        k.tt("dve", var[:, :n], mean[:, :n], mean[:, :n], ALU.mult, ["ln_mean"], ["ln_var"])
        k.tt("dve", var[:, :n], psv[:, :n], var[:, :n], ALU.subtract, [("ps", 7), "ln_var"], ["ln_var"])
        k.ts("dve", var[:, :n], var[:, :n], LN_EPS, None, ALU.add, None, ["ln_var"], ["ln_var"])
        k.act(var[:, :n], var[:, :n], AF.Sqrt, ["ln_var"], ["ln_var"])
        k.recip(var[:, :n], var[:, :n], ["ln_var"], ["ln_var"])
        go, _ = _vec_layout()[gname]
        bo, _ = _vec_layout()[bname]
        for c in range(8):
            e = "dve" if c % 2 == 0 else "pool"
            k.tt(e, sq[:, c, :n], r[:, c, :n], mean[:, :n], ALU.subtract, [rkey, "ln_mean"], [("ln_sq", c)])
            k.tt(e, sq[:, c, :n], sq[:, c, :n], var[:, :n], ALU.mult, [("ln_sq", c), "ln_var"], [("ln_sq", c)])
            k.ts(e, outb[:, c, :n], sq[:, c, :n],
                 self.vecs[:, go + layer * 8 + c:go + layer * 8 + c + 1],
                 self.vecs[:, bo + layer * 8 + c:bo + layer * 8 + c + 1],
                 ALU.mult, ALU.add, [("ln_sq", c), "vecs"], [("ln_sq", c)] if inplace else [okey])
        return [("ln_sq", c) for c in range(8)]

    def phase_ffn(self, layer, src, dst, last):
        k = self.k
        k.fence()
        FB = 512
        fblocks = [(i * FB, FB) for i in range(TL // FB)] + ([] if last else [(TL, TC)])
        nblocks = len(fblocks)
        with ExitStack() as es:
            def sb(name, shape, dt):
                return es.enter_context(self._sbt(name, list(shape), dt))
            wg = sb("wg", [128, 8, FF], BF16)
            wu = sb("wu", [128, 8, FF], BF16)
            wd = sb("wd", [128, NFF, D], BF16)
            xb = [sb("xb%d" % i, [128, 8, FB], F32) for i in range(2)]
            h2 = sb("h2", [128, 8, FB], BF16)
            abuf = sb("abuf", [128, NFF * FB // 2], F32)
            a = abuf[:, :].bitcast(BF16).rearrange("p (j n) -> p j n", n=FB)
            lnsq = abuf[:, 0:8 * FB].rearrange("p (c n) -> p c n", n=FB)
            sg = [sb("sg%d" % i, [128, FB], F32) for i in range(2)]
            tmp = {"sq": lnsq, "mean": sb("lnmean", [128, FB], F32), "var": sb("lnvar", [128, FB], F32)}
            akeys = [("a", j) for j in range(NFF)]
            for g_ in range(2):
                c0_, c1_ = g_ * (FF // 2), (g_ + 1) * (FF // 2)
                for kc in range(8):
                    k.dma("pool", wg[:, kc, c0_:c1_], self.w_gate[layer, kc * 128:(kc + 1) * 128, c0_:c1_], [], [("wg", g_)], append=True)
                    k.dma("pool", wu[:, kc, c0_:c1_], self.w_up[layer, kc * 128:(kc + 1) * 128, c0_:c1_], [], [("wu", g_)], append=True)
            for j in range(NFF):
                k.dma("pool", wd[:, j, :], self.w_down[layer, j * 128:(j + 1) * 128, :], [], [("wd", j // 6)], append=True)
            srcv = src.rearrange("(c p) t -> p c t", p=128)
            dstv = dst.rearrange("(c p) t -> p c t", p=128)
            dname = "out" if last else "xs0"
            k.dma("sp", xb[0][:, :, :fblocks[0][1]], srcv[:, :, 0:fblocks[0][1]], [], [("xb", 0)])
            for bi in range(nblocks):
                t0, n = fblocks[bi]
                s = blk_set(t0)
                x = xb[bi % 2]
                xk = ("xb", bi % 2)
                if bi + 1 < nblocks:
                    t1, n1 = fblocks[bi + 1]
                    k.dma("sp", xb[(bi + 1) % 2][:, :, :n1], srcv[:, :, t1:t1 + n1], [], [("xb", (bi + 1) % 2)])
                for c in range(8):
                    e = "dve" if c % 2 == 0 else "pool"
                    k.ts(e, h2[:, c, :n], x[:, c, :n], self.modcol(4, c, s, True), self.modcol(3, c, s),
                         ALU.mult, ALU.add, [xk, "mod", "modp"], [("h2", c)])
                for j in range(NFF):
                    pg, pu = self.psb[(2 * j) % 4], self.psb[(2 * j + 1) % 4]
                    kg, ku = ("ps", (2 * j) % 4), ("ps", (2 * j + 1) % 4)
                    for kc in range(8):
                        k.mm(pg[:, :n], wg[:, kc, j * 128:(j + 1) * 128], h2[:, kc, :n], kc == 0, kc == 7,
                             [("wg", j // 11), ("h2", kc)], [kg])
                    for kc in range(8):
                        k.mm(pu[:, :n], wu[:, kc, j * 128:(j + 1) * 128], h2[:, kc, :n], kc == 0, kc == 7,
                             [("wu", j // 11), ("h2", kc)], [ku])
                    sgb = sg[j % 2]
                    k.act(sgb[:, :n], pg[:, :n], AF.Silu, [kg], [("sg", j % 2)])
                    k.tt("dve", a[:, j, :n], sgb[:, :n], pu[:, :n], ALU.mult, [("sg", j % 2), ku], [("a", j)])
                for c in range(8):
                    k.ts("pool", x[:, c, :n], x[:, c, :n], ALPHA, None, ALU.mult, None, [xk, ("h2", c)], [xk])
                for c in range(8):
                    pd = self.psb[4 + c % 2]
                    kd = ("ps", 4 + c % 2)
                    for j in range(NFF):
                        k.mm(pd[:, :n], wd[:, j, c * 128:(c + 1) * 128], a[:, j, :n], j == 0, j == NFF - 1,
                             [("wd", j // 6), ("a", j)], [kd])
                    k.stt("dve", x[:, c, :n], pd[:, :n], self.modcol(5, c, s), x[:, c, :n], ALU.mult, ALU.add,
                          [kd, "mod", xk], [xk])
                okeys = self.ln_block(x, n, "ln2_g", "ln2_b", layer, None, xk, None, tmp, extra_wr=akeys)
                k.dma("sp", dstv[:, :, t0:t0 + n], lnsq[:, :, :n], okeys + akeys, [(dname, bi)])

    def mixer_out(self, es, layer, w_out_d, og, src, dst, nblocks, dname, ogkeys, og_dram=None):
        k = self.k

        def sb(name, shape, dt):
            return es.enter_context(self._sbt(name, list(shape), dt))
        MB = 512
        mblocks = [(i * MB, MB) for i in range(TL // MB)] + ([(TL, TC)] if nblocks == 9 else [])
        nblocks = len(mblocks)
        wo = sb("wo", [128, 8, D], BF16)
        xb = [sb("mo_xb%d" % i, [128, 8, MB], F32) for i in range(2)]
        tmp = {"sq": sb("mo_lnsq", [128, 8, MB], F32), "mean": sb("mo_lnmean", [128, MB], F32),
               "var": sb("mo_lnvar", [128, MB], F32)}
        for kc in range(8):
            k.dma("pool", wo[:, kc, :], w_out_d[kc * 128:(kc + 1) * 128, :], [], ["wo"], append=True)
        srcv = src.rearrange("(c p) t -> p c t", p=128)
        dstv = dst.rearrange("(c p) t -> p c t", p=128)
        n0 = mblocks[0][1]
        k.dma("sp", xb[0][:, :, :n0], srcv[:, :, 0:n0], [], [("mo_xb", 0)])
        if og_dram is not None:
            ogb = [sb("mo_ogb%d" % i, [128, 8, MB], BF16) for i in range(2)]
            ogv = og_dram.rearrange("h p t -> p h t")
            k.dma("sp", ogb[0][:, :, :n0], ogv[:, :, 0:n0], ["dn_og"], [("mo_ogb", 0)])
        for bi in range(nblocks):
            t0, n = mblocks[bi]
            s_ = blk_set(t0)
            x = xb[bi % 2]
            xk = ("mo_xb", bi % 2)
            if bi + 1 < nblocks:
                t1, n1 = mblocks[bi + 1]
                k.dma("sp", xb[(bi + 1) % 2][:, :, :n1], srcv[:, :, t1:t1 + n1], [], [("mo_xb", (bi + 1) % 2)])
                if og_dram is not None:
                    k.dma("sp", ogb[(bi + 1) % 2][:, :, :n1], ogv[:, :, t1:t1 + n1], ["dn_og"], [("mo_ogb", (bi + 1) % 2)])
            if og_dram is not None:
                og = ogb[bi % 2]
                ogoff = 0
                okf = lambda h, bi_=bi: [("mo_ogb", bi_ % 2)]
            else:
                ogoff = t0
                okf = lambda h, t0_=t0, n_=n: [k_ for i_ in range(n_ // 256) for k_ in ogkeys(h, t0_ // 256 + i_)]
            for c in range(8):
                k.ts("pool", x[:, c, :n], x[:, c, :n], ALPHA, None, ALU.mult, None, [xk], [xk])
            for c in range(8):
                pd = self.psb[4 + c % 2]
                kd = ("ps", 4 + c % 2)
                for h in range(8):
                    k.mm(pd[:, :n], wo[:, h, c * 128:(c + 1) * 128], og[:, h, ogoff:ogoff + n], h == 0, h == 7,
                         ["wo"] + okf(h), [kd])
                k.stt("dve", x[:, c, :n], pd[:, :n], self.modcol(2, c, s_), x[:, c, :n], ALU.mult, ALU.add,
                      [kd, "mod", xk], [xk])
            okeys = self.ln_block(x, n, "ln1_g", "ln1_b", layer, None, xk, None, tmp)
            k.dma("sp", dstv[:, :, t0:t0 + n], tmp["sq"][:, :, :n], okeys, [(dname, bi)])

    def load_h(self, es, src, hT, nblocks):
        k = self.k
        xb = [es.enter_context(self._sbt("lh_xb%d" % i, [128, 8, NB], F32)) for i in range(2)]
        srcv = src.rearrange("(c p) t -> p c t", p=128)
        k.dma("sp", xb[0][:, :, :], srcv[:, :, 0:NB], [], [("lh_xb", 0)])
        for bi in range(nblocks):
            t0, n = BLOCKS[bi]
            s_ = blk_set(t0)
            if bi + 1 < nblocks:
                t1, _ = BLOCKS[bi + 1]
                k.dma("sp", xb[(bi + 1) % 2][:, :, :], srcv[:, :, t1:t1 + NB], [], [("lh_xb", (bi + 1) % 2)])
            for c in range(8):
                e = "dve" if c % 2 == 0 else "pool"
                k.ts(e, hT[:, c, t0:t0 + n], xb[bi % 2][:, c, :], self.modcol(1, c, s_, True), self.modcol(0, c, s_),
                     ALU.mult, ALU.add, [("lh_xb", bi % 2), "mod", "modp"], [("hT", bi)])

    def phase_da(self, layer, src, dst):
        k = self.k
        k.fence()
        lam_init = 0.8 - 0.6 * math.exp(-0.3 * layer)
        NKT = T // 128
        QB = 512
        with ExitStack() as es_outer:
            def sbo(name, shape, dt):
                return es_outer.enter_context(self._sbt(name, list(shape), dt))
            qT = sbo("da_qT", [128, 8, TL], BF16)
            og = qT
            with ExitStack() as es_mid:
                def sbm(name, shape, dt):
                    return es_mid.enter_context(self._sbt(name, list(shape), dt))
                kT = sbm("da_kT", [128, 8, T], BF16)
                vt = sbm("da_vt", [128, NKT, 8, 130], BF16)
                lam = sbm("da_lam", [128, 4], F32)
                lo, _ = _vec_layout()["da_lambda"]
                k.tt("dve", lam[0:64, 0:1], self.vecs[0:64, lo:lo + 1], self.vecs[0:64, lo + 1:lo + 2], ALU.mult, ["vecs"], ["lam"])
                k.tt("dve", lam[0:64, 1:2], self.vecs[0:64, lo + 2:lo + 3], self.vecs[0:64, lo + 3:lo + 4], ALU.mult, ["vecs", "lam"], ["lam"])
                pl = self.psb[0]
                k.mm(pl[:, 0:2], self.ones[0:64, :], lam[0:64, 0:2], True, True, ["ones", "lam"], [("ps", 0)])
                k.act(lam[:, 2:4], pl[:, 0:2], AF.Exp, [("ps", 0), "lam"], ["lam"])
                k.tt("dve", lam[:, 0:1], lam[:, 2:3], lam[:, 3:4], ALU.subtract, ["lam"], ["lam"])
                k.ts("dve", lam[:, 1:2], lam[:, 0:1], lam_init, -1.0, ALU.add, ALU.mult, ["lam"], ["lam"])
                k.memset("pool", vt[:, :, :, 128:130], 1.0, ["vt_ones"])
                self.tap("tap_lam", lam[:, :], [128, 4], ["lam", "vt_ones"])
                if getattr(self, "stop_after", "") == "lam":
                    return
                with ExitStack() as es:
                    def sb(name, shape, dt):
                        return es.enter_context(self._sbt(name, list(shape), dt))
                    hT = sb("da_hT", [128, 8, T], BF16)
                    w = sb("da_w", [128, 8, D], BF16)
                    cos = sb("da_cos", [128, TL], F32)
                    sin = sb("da_sin", [128, TL], F32)
                    rot = sb("da_rot", [128, 128], F32)
                    rotb = sb("da_rotb", [128, 128], BF16)
                    xbf = [sb("da_xbf%d" % i, [128, 512], BF16) for i in range(2)]
                    t1b = [sb("da_t1%d" % i, [128, 512], F32) for i in range(2)]
                    t2b = [sb("da_t2%d" % i, [128, 512], F32) for i in range(2)]
                    k.dma("sp", cos[:, :], self.const_d["ropecos"][:, :], [], ["cos"])
                    k.dma("sp", sin[:, :], self.const_d["ropesin"][:, :], [], ["sin"])
                    k.dma("sp", rot[:, :], self.const_d["roperot"][:, :], [], ["rot"])
                    k.copy("dve", rotb[:, :], rot[:, :], ["rot"], ["rotb"])
                    self.load_h(es, src, hT, 9)
                    hkeys = [("hT", b) for b in range(9)]
                    it = 0
                    self.tap("tap_hT", hT[:, 0, :], [128, T], hkeys + ["rotb", "cos", "sin"], BF16)
                    if getattr(self, "stop_after", "") == "loadh":
                        return
                    for part in range(2):
                        for kc in range(8):
                            k.dma("pool", w[:, kc, :], self.da_w_in[kc * 128:(kc + 1) * 128, part * D:(part + 1) * D],
                                  [], ["da_w"], append=(kc > 0))
                        dstT = qT if part == 0 else kT
                        dk = "qT" if part == 0 else "kT"
                        tblocks = [(i * 512, 512) for i in range(4)] + ([(TL, TC)] if part == 1 else [])
                        for h in range(8):
                            for (t0, n) in tblocks:
                                px = self.psb[it % 2]
                                kx = ("ps", it % 2)
                                for kc in range(8):
                                    k.mm(px[:, :n], w[:, kc, h * 128:(h + 1) * 128], hT[:, kc, t0:t0 + n], kc == 0, kc == 7,
                                         ["da_w"] + hkeys, [kx])
                                wkeys = [("qT", h, t0 // 256), ("qT", h, t0 // 256 + 1)] if part == 0 else [("kT", h)]
                                if t0 >= TL:
                                    k.copy("act", dstT[:, h, t0:t0 + n], px[:, :n], [kx], wkeys)
                                else:
                                    xb_ = xbf[it % 2]
                                    pr = self.psb[2 + it % 2]
                                    kr = ("ps", 2 + it % 2)
                                    k.copy("act", xb_[:, :n], px[:, :n], [kx], [("xbf", it % 2)])
                                    k.mm(pr[:, :n], rotb[:, :], xb_[:, :n], True, True, ["rotb", ("xbf", it % 2)], [kr])
                                    k.tt("dve", t1b[it % 2][:, :n], px[:, :n], cos[:, t0:t0 + n], ALU.mult, [kx, "cos"], [("t1", it % 2)])
                                    k.tt("dve", t2b[it % 2][:, :n], pr[:, :n], sin[:, t0:t0 + n], ALU.mult, [kr, "sin"], [("t2", it % 2)])
                                    k.tt("pool", dstT[:, h, t0:t0 + n], t1b[it % 2][:, :n], t2b[it % 2][:, :n], ALU.add,
                                         [("t1", it % 2), ("t2", it % 2)], wkeys)
                                it += 1
                    if getattr(self, "stop_after", "") == "qk":
                        self.tap("tap_qT", qT[:, 0, :], [128, TL], [("qT", 0, b_) for b_ in range(8)], BF16)
                        self.tap("tap_kT", kT[:, 0, :], [128, T], ("kT", 0), BF16)
                        return
                    for kc in range(8):
                        k.dma("pool", w[:, kc, :], self.da_w_in[kc * 128:(kc + 1) * 128, 2 * D:3 * D], [], ["da_w"], append=(kc > 0))
                    for tt_ in range(NKT):
                        for half in range(2):
                            pv = self.psb[4 + it % 2]
                            kv = ("ps", 4 + it % 2)
                            for kc in range(8):
                                k.mm(pv[:, :], hT[:, kc, tt_ * 128:(tt_ + 1) * 128], w[:, kc, half * 512:(half + 1) * 512],
                                     kc == 0, kc == 7, ["da_w"] + hkeys, [kv])
                            e = "act" if it % 2 == 0 else "dve"
                            k.copy(e, vt[:, tt_, half * 4:(half + 1) * 4, 0:128], pv[:, :].rearrange("p (h d) -> p h d", d=128),
                                   [kv], [("vt", tt_)])
                            it += 1
                self.tap("tap_qT", qT[:, 0, :], [128, TL], [("qT", 0, b_) for b_ in range(8)], BF16)
                self.tap("tap_kT", kT[:, 0, :], [128, T], ("kT", 0), BF16)
                self.tap("tap_vt", vt[:, :, 0, :], [128, NKT, 130], [("vt", t_) for t_ in range(NKT)] + ["vt_ones"], BF16)
                if getattr(self, "stop_after", "") == "proj":
                    return
                k.fence()
                with ExitStack() as es:
                    def sb(name, shape, dt):
                        return es.enter_context(self._sbt(name, list(shape), dt))
                    PT = [[sb("da_PT%d_%d" % (i, c), [128, NKT, QB], BF16) for c in range(2)] for i in range(2)]
                    small = sb("da_small", [128, 16], F32)
                    o1 = [sb("da_o1_%d" % i, [128, 128], F32) for i in range(2)]
                    o2 = [sb("da_o2_%d" % i, [128, 128], F32) for i in range(2)]
                    onb = [sb("da_onb_%d" % i, [128, 128], BF16) for i in range(2)]
                    sqj = sb("da_sqj", [128, 128], F32)
                    go_, _ = _vec_layout()["da_subln_g"]
                    gsc = sb("da_gsc", [128, 1], F32)
                    k.ts("dve", gsc[:, :], self.vecs[:, go_:go_ + 1], 1.0 - lam_init, None, ALU.mult, None, ["vecs"], ["gsc"])
                    vkeys = [("vt", t_) for t_ in range(NKT)] + ["vt_ones"]
                    mhalf = sb("da_mhalf", [128, 1], F32)
                    k.memset("pool", mhalf[:, :], -0.5, ["mhalf"])

                    def pv_post(h, qb, pset, itp, jt0):
                        q0 = qb * QB
                        for qt in range(QB // 128):
                            jt = jt0 + qt
                            pos = [self.psb[3 + 2 * (jt % 2)], self.psb[4 + 2 * (jt % 2)]]
                            kos = [("ps", 3 + 2 * (jt % 2)), ("ps", 4 + 2 * (jt % 2))]
                            for c in range(2):
                                for kt in range(NKT):
                                    k.mm(pos[c][:, 0:129], pset[c][:, kt, qt * 128:(qt + 1) * 128], vt[:, kt, h, 0:129],
                                         kt == 0, kt == NKT - 1, [("PT", itp, c)] + vkeys, [kos[c]])
                                    if kt % 2 == 1:
                                        yield
                            j = jt % 2
                            sm = small[:, j * 8:(j + 1) * 8]
                            smk = lambda i_, j=j: ("sm", j, i_)
                            k.recip(sm[:, 0:1], pos[0][:, 128:129], [kos[0]], [smk(0)])
                            k.recip(sm[:, 1:2], pos[1][:, 128:129], [kos[1]], [smk(1)])
                            yield
                            k.tt("dve", sm[:, 1:2], sm[:, 1:2], lam[:, 1:2], ALU.mult, [smk(1), "lam"], [smk(1)])
                            k.ts("dve", o2[j][:, :], pos[1][:, 0:128], sm[:, 1:2], None, ALU.mult, None, [kos[1], smk(1)], [("o2", j)])
                            yield
                            k.stt("dve", o1[j][:, :], pos[0][:, 0:128], sm[:, 0:1], o2[j][:, :], ALU.mult, ALU.add,
                                  [kos[0], smk(0), ("o2", j)], [("o1", j)])
                            yield
                            k._emit("act", lambda g, a_=sqj, b_=o1[j], c_=sm: g.activation(out=a_[:, :], in_=b_[:, :], func=AF.Square,
                                                                                     accum_out=c_[:, 2:3]),
                                    [("o1", j)], ["sqj", smk(2)])
                            yield
                            k.ts("dve", sm[:, 3:4], sm[:, 2:3], 1.0 / 128.0, RMS_EPS, ALU.mult, ALU.add, [smk(2)], [smk(3)])
                            k.tt("pool", sm[:, 3:4], sm[:, 3:4], mhalf[:, :], ALU.pow, [smk(3), "mhalf"], [smk(3)])
                            yield
                            k.ts("dve", onb[j][:, :], o1[j][:, :], sm[:, 3:4], None, ALU.mult, None, [("o1", j), smk(3)], [("onb", j)])
                            yield
                            ptr = self.psbf[1]
                            ktr = ("ps", 7)
                            k.tr(ptr[:, 0:128], onb[j][:, :], self.identb[:, :], [("onb", j), "identb"], [ktr])
                            yield
                            k.ts("dve", og[:, h, q0 + qt * 128:q0 + (qt + 1) * 128], ptr[:, 0:128], gsc[:, 0:1], None,
                                 ALU.mult, None, [ktr, "gsc"], [("qT", h, (q0 + qt * 128) // 256)])
                            yield

                    it = 0
                    jt = 0
                    pending = None
                    for h in range(8):
                        for qb in range(TL // QB):
                            q0 = qb * QB
                            pset = PT[it % 2]
                            for kt in range(NKT):
                                for c in range(2):
                                    psc = self.psb[(2 * kt + c) % 3]
                                    ksc = ("ps", (2 * kt + c) % 3)
                                    k.mm(psc[:, :QB], kT[c * 64:(c + 1) * 64, h, kt * 128:(kt + 1) * 128],
                                         qT[c * 64:(c + 1) * 64, h, q0:q0 + QB], True, True,
                                         [("kT", h)] + [("qT", h, (q0 + i_ * 256) // 256) for i_ in range(QB // 256)], [ksc])
                                    k.act(pset[c][:, kt, :], psc[:, :QB], AF.Exp, [ksc], [("PT", it % 2, c)], scale=0.125,
                                          bias=-16.0)
                                if pending is not None:
                                    for _ in range(4):
                                        if next(pending, "done") == "done":
                                            pending = None
                                            break
                            if pending is not None:
                                for _ in pending:
                                    pass
                            pending = pv_post(h, qb, pset, it % 2, jt)
                            jt += QB // 128
                            it += 1
                    for _ in pending:
                        pass
            self.tap("tap_og", og[:, 0, :], [128, TL], [("qT", 0, b_) for b_ in range(8)], BF16)
            if getattr(self, "stop_after", "") == "attn":
                return
            k.fence()
            with ExitStack() as es:
                self.mixer_out(es, layer, self.da_w_out, og, src, dst, 8, "xs1",
                               lambda h, bi: [("qT", h, bi)])


    def phase_dn(self, layer, src, dst):
        k = self.k
        nc = self.nc
        k.fence()
        NCK = T // 64
        TB = [(i * 512, 512) for i in range(4)] + [(TL, TC)]
        SEGS = [(0, TL), (TL, T)]
        dq = [k.dram("dn_%s" % nm, [8, 128, T], BF16) for nm in ("q", "k", "v", "z")]
        dn_og = k.dram("dn_og", [8, 128, T], BF16)
        co, _ = _vec_layout()["dn_conv"]
        with ExitStack() as es_g:
            def sbg(name, shape, dt):
                return es_g.enter_context(self._sbt(name, list(shape), dt))
            Gfm = [sbg("dn_G%d" % d_, [8, T], F32) for d_ in range(2)]
            Bfm = [sbg("dn_B%d" % d_, [8, T], F32) for d_ in range(2)]
            Gtm2 = sbg("dn_Gtm2", [128, NCK, 8], F32)
            Btm2 = sbg("dn_Btm2", [128, NCK, 8], F32)
            sel = sbg("dn_sel_sb", [8, 8, 128], F32)
            masks = sbg("dn_masks_sb", [128, 3, 64], F32)
            ident2 = sbg("dn_ident2_sb", [128, 64], F32)
            k.dma("sp", sel[:, :, :], self.const_d["dn_sel"].rearrange("k (r m) -> k r m", m=128), [], ["dn_sel"])
            k.dma("sp", masks[:, :, :], self.const_d["dn_masks2"].rearrange("p (a f) -> p a f", f=64), [], ["dn_masks"])
            k.dma("sp", ident2[:, :], self.const_d["dn_ident2"][:, :], [], ["dn_ident2"])
            with ExitStack() as es:
                def sb(name, shape, dt):
                    return es.enter_context(self._sbt(name, list(shape), dt))
                hT = sb("dn_hT", [128, 8, T], BF16)
                wbuf = [sb("dn_w%d" % i, [128, 8, D], BF16) for i in range(2)]
                wab = sb("dn_wab", [128, 8, 32], BF16)
                pre = [sb("dn_pre%d" % i, [128, T], BF16) for i in range(2)]
                dg = [sb("dn_dg%d" % i, [128, 5, 128], BF16) for i in range(2)]
                cv = [sb("dn_cv%d" % i, [128, T], F32) for i in range(2)]
                sqb = sb("dn_sqb", [128, T], BF16)
                rn = [sb("dn_rn%d" % i, [128, 512], F32) for i in range(2)]
                outb = [sb("dn_outb%d" % i, [128, T], BF16) for i in range(3)]
                gtmp = [sb("dn_gtmp%d" % i, [8, T], F32) for i in range(2)]
                nA = sb("dn_nA", [8, 2], F32)
                self.load_h(es, src, hT, 9)
                hkeys = [("hT", b) for b in range(9)]
                for kc in range(8):
                    k.dma("pool", wab[:, kc, :], self.dn_w_in[kc * 128:(kc + 1) * 128, 4096:4128], [], ["dn_wab"], append=(kc > 0))
                alo, _ = _vec_layout()["dn_a_log"]
                dbo, _ = _vec_layout()["dn_dt_bias"]
                k.act(nA[:, :], self.vecs[0:8, alo:alo + 2], AF.Exp, ["vecs"], ["dn_nA"])
                k.ts("dve", nA[:, :], nA[:, :], -1.0, None, ALU.mult, None, ["dn_nA"], ["dn_nA"])
                it = 0
                for d_ in range(2):
                    for (t0, n) in TB:
                        pa = self.psb[it % 2]
                        ka = ("ps", it % 2)
                        for kc in range(8):
                            k.mm(pa[0:8, :n], wab[:, kc, d_ * 8:d_ * 8 + 8], hT[:, kc, t0:t0 + n], kc == 0, kc == 7,
                                 ["dn_wab"] + hkeys, [ka])
                        k.act(Gfm[d_][:, t0:t0 + n], pa[0:8, :n], AF.Exp, [ka, "vecs"], [("Gfm", d_)],
                              bias=self.vecs[0:8, dbo + d_:dbo + d_ + 1])
                        it += 1
                        pb = self.psb[it % 2]
                        kb_ = ("ps", it % 2)
                        for kc in range(8):
                            k.mm(pb[0:8, :n], wab[:, kc, 16 + d_ * 8:16 + d_ * 8 + 8], hT[:, kc, t0:t0 + n], kc == 0, kc == 7,
                                 ["dn_wab"] + hkeys, [kb_])
                        k.act(Bfm[d_][:, t0:t0 + n], pb[0:8, :n], AF.Sigmoid, [kb_], [("Bfm", d_)])
                        it += 1
                    k.ts("dve", Gfm[d_][:, :], Gfm[d_][:, :], 1.0, None, ALU.add, None, [("Gfm", d_)], [("Gfm", d_)])
                    k.act(Gfm[d_][:, :], Gfm[d_][:, :], AF.Ln, [("Gfm", d_)], [("Gfm", d_)])
                    k.ts("dve", Gfm[d_][:, :], Gfm[d_][:, :], nA[:, d_:d_ + 1], None, ALU.mult, None, [("Gfm", d_), "dn_nA"], [("Gfm", d_)])
                    cur = Gfm[d_]
                    ck = ("Gfm", d_)
                    gi = 0
                    for sft in (1, 2, 4, 8, 16, 32):
                        nxt = gtmp[gi % 2]
                        nk = ("gtmp", gi % 2)
                        k.copy("pool", nxt[:, :], cur[:, :], [ck], [nk])
                        cv3 = cur[:, :].rearrange("p (c f) -> p c f", f=64)
                        nv3 = nxt[:, :].rearrange("p (c f) -> p c f", f=64)
                        if d_ == 0:
                            k.tt("dve", nv3[:, :, sft:], nv3[:, :, sft:], cv3[:, :, :64 - sft], ALU.add, [ck, nk], [nk])
                        else:
                            k.tt("dve", nv3[:, :, :64 - sft], nv3[:, :, :64 - sft], cv3[:, :, sft:], ALU.add, [ck, nk], [nk])
                        cur, ck = nxt, nk
                        gi += 1
                    k.copy("pool", Gfm[d_][:, :], cur[:, :], [ck], [("Gfm", d_)])
                    for (tile_, tmt, nm) in ((Gfm[d_], Gtm2, "Gtm2"), (Bfm[d_], Btm2, "Btm2")):
                        pt = self.psb[2 + it % 2]
                        kt_ = ("ps", 2 + it % 2)
                        hs = slice(d_ * 64, (d_ + 1) * 64)
                        for c in range(NCK):
                            k.mm(pt[hs, c * 8:(c + 1) * 8], tile_[0:8, c * 64:(c + 1) * 64], self.ident[0:8, 0:8], True, True,
                                 [("Gfm", d_), ("Bfm", d_), "ident"], [kt_])
                        k.copy("act", tmt[hs, :, :], pt[hs, 0:NCK * 8].rearrange("p (c r) -> p c r", r=8), [kt_], [(nm, d_)])
                        it += 1
                items = [(part, h) for part in range(4) for h in range(8)]
                NI = len(items)
                itc = [0]

                def load_w(part):
                    wt = wbuf[part % 2]
                    for kc in range(8):
                        k.dma("pool", wt[:, kc, :], self.dn_w_in[kc * 128:(kc + 1) * 128, part * D:(part + 1) * D],
                              [], [("dn_w", part % 2)], append=(kc > 0))

                def S1(i):
                    part, h = items[i]
                    if h == 0 and part + 1 < 4:
                        load_w(part + 1)
                    wt = wbuf[part % 2]
                    pj = i % 2
                    ob_ = outb[i % 3]
                    okey = ("dn_outb", i % 3)
                    for (t0, n) in TB:
                        px = self.psb[itc[0] % 2]
                        kx = ("ps", itc[0] % 2)
                        for kc in range(8):
                            k.mm(px[:, :n], wt[:, kc, h * 128:(h + 1) * 128], hT[:, kc, t0:t0 + n], kc == 0, kc == 7,
                                 [("dn_w", part % 2)] + hkeys, [kx])
                        if part == 3:
                            k.act(ob_[:, t0:t0 + n], px[:, :n], AF.Silu, [kx], [okey])
                        else:
                            k.copy("act", pre[pj][:, t0:t0 + n], px[:, :n], [kx], [("dn_pre", pj)])
                        itc[0] += 1

                def S2(i):
                    part, h = items[i]
                    if part == 3:
                        return
                    pj = i % 2
                    ob_ = outb[i % 3]
                    okey = ("dn_outb", i % 3)
                    ch = part * 8 + h
                    cvt = cv[pj]
                    ckey = ("dn_cv", pj)
                    dgt = dg[pj]
                    for j in range(5):
                        e = "pool" if j % 2 == 0 else "dve"
                        k.ts(e, dgt[:, j, :], self.identb[:, :], self.vecs[:, co + ch * 5 + j:co + ch * 5 + j + 1], None, ALU.mult, None,
                             ["identb", "vecs"], [("dn_dg", pj, j)])
                    for (t0, n) in TB:
                        s0, s1 = (0, TL) if t0 < TL else (TL, T)
                        pc = self.psb[4 + itc[0] % 2]
                        kc_ = ("ps", 4 + itc[0] % 2)
                        taps = [2, 0, 1, 3, 4]
                        for ti, j in enumerate(taps):
                            sh = j - 2
                            lo, hi = max(t0, s0 - sh), min(t0 + n, s1 - sh)
                            k.mm(pc[:, lo - t0:hi - t0], dgt[:, j, :], pre[pj][:, lo + sh:hi + sh], ti == 0, ti == 4,
                                 [("dn_dg", pj, j), ("dn_pre", pj)], [kc_])
                        if part == 2:
                            k.act(ob_[:, t0:t0 + n], pc[:, :n], AF.Silu, [kc_], [okey])
                        else:
                            k.act(cvt[:, t0:t0 + n], pc[:, :n], AF.Silu, [kc_], [ckey])
                        itc[0] += 1
                    if part == 0 and h == 0:
                        self.tap("tap_pre", pre[pj][:, :], [128, T], ("dn_pre", pj), BF16)
                        self.tap("tap_cv", cvt[:, :], [128, T], ckey)
                        self.tap("tap_hT0", hT[:, 0, :], [128, T], hkeys, BF16)

                def S3(i):
                    part, h = items[i]
                    pj = i % 2
                    ob_ = outb[i % 3]
                    okey = ("dn_outb", i % 3)
                    if part < 2:
                        cvt = cv[pj]
                        ckey = ("dn_cv", pj)
                        k.tt("pool", sqb[:, :], cvt[:, :], cvt[:, :], ALU.mult, [ckey], ["dn_sqb"])
                        for bi_, (t0, n) in enumerate(TB):
                            pn = self.psb[2 + itc[0] % 2]
                            kn = ("ps", 2 + itc[0] % 2)
                            k.mm(pn[:, :n], self.onesb[:, :], sqb[:, t0:t0 + n], True, True, ["onesb", "dn_sqb"], [kn])
                            r_ = rn[itc[0] % 2]
                            rk = ("dn_rn", itc[0] % 2)
                            k.ts("dve", r_[:, :n], pn[:, :n], RMS_EPS, None, ALU.add, None, [kn], [rk])
                            k.act(r_[:, :n], r_[:, :n], AF.Sqrt, [rk], [rk])
                            k.recip(r_[:, :n], r_[:, :n], [rk], [rk])
                            if part == 0:
                                k.stt("dve", ob_[:, t0:t0 + n], cvt[:, t0:t0 + n], 128.0 ** -0.5, r_[:, :n], ALU.mult, ALU.mult,
                                      [ckey, rk], [okey])
                            else:
                                k.tt("dve", ob_[:, t0:t0 + n], cvt[:, t0:t0 + n], r_[:, :n], ALU.mult, [ckey, rk], [okey])
                            itc[0] += 1
                    k.dma("sp", dq[part][h], ob_[:, :], [okey], [("dq", part, h)])

                load_w(0)
                for i in range(NI + 2):
                    if i < NI:
                        S1(i)
                    if 0 <= i - 1 < NI:
                        S2(i - 1)
                    if 0 <= i - 2 < NI:
                        S3(i - 2)
                self.tap("tap_Gfm0", Gfm[0][:, :], [8, T], ("Gfm", 0))
                self.tap("tap_Gfm1", Gfm[1][:, :], [8, T], ("Gfm", 1))
                self.tap("tap_Bfm0", Bfm[0][:, :], [8, T], ("Bfm", 0))
            if getattr(self, "stop_after", "") == "dnproj":
                self.dq = dq
                return
            k.fence()
            with ExitStack() as es:
                def sb(name, shape, dt):
                    return es.enter_context(self._sbt(name, list(shape), dt))
                qT = sb("dn_qT", [128, T], BF16)
                kT = sb("dn_kT", [128, T], BF16)
                vT = sb("dn_vT", [128, T], BF16)
                zs = qT
                ktm = sb("dn_ktm", [128, NCK, 128], BF16)
                vtm = sb("dn_vtm", [128, NCK, 128], BF16)
                oacc = [sb("dn_oacc%d" % d_, [128, T], F32) for d_ in range(2)]
                U = sb("dn_U", [128, NCK, 128], BF16)
                KD = sb("dn_KD", [128, NCK, 128], BF16)
                QK = sb("dn_QK", [128, T], BF16)
                WT = [sb("dn_WT%d" % d_, [128, T], BF16) for d_ in range(2)]
                QD = [sb("dn_QD%d" % d_, [128, T], BF16) for d_ in range(2)]
                DS = [sb("dn_DS%d" % d_, [128, NCK], F32) for d_ in range(2)]
                coef = sb("dn_coef", [128, NCK], F32)
                dtm = sb("dn_dtm", [128, NCK], F32)
                glast = sb("dn_glast", [128, NCK], F32)
                E = sb("dn_E", [128, 512], F32)
                F1 = sb("dn_F1", [128, 512], F32)
                gT_i = sb("dn_gTi", [128, 512], F32)
                gT_s = sb("dn_gTs", [128, 512], F32)
                g_s = sb("dn_gs", [128, 512], F32)
                kbT = [sb("dn_kbT%d" % d_, [128, 512], BF16) for d_ in range(2)]
                eG = sb("dn_eG", [128, 512], F32)
                PbS = [[sb("dn_P%d_%d" % (s_, i), [128, 512], F32) for i in range(2)] for s_ in range(2)]
                QbS = [[sb("dn_Q%d_%d" % (s_, i), [128, 512], F32) for i in range(2)] for s_ in range(2)]
                XfS = [sb("dn_Xf%d" % s_, [128, 512], F32) for s_ in range(2)]
                XbS = [sb("dn_Xb%d" % s_, [128, 512], BF16) for s_ in range(2)]
                vb = sb("dn_vb", [128, 8, 128], BF16)
                kg = sb("dn_kg", [128, 8, 128], BF16)
                S = [sb("dn_S%d" % d_, [128, 128], F32) for d_ in range(2)]
                Sb = [sb("dn_Sb%d" % d_, [128, 128], BF16) for d_ in range(2)]
                vn = sb("dn_vn", [128, 128], BF16)
                osum = eG
                osq = kbT[0]
                orn = sb("dn_orn", [128, 512], F32)
                ogt = vT
                ngo, _ = _vec_layout()["dn_norm_g"]
                HS = [slice(0, 64), slice(64, 128)]
                LASTF = [63, 0]
                pi = 0
                for h in range(8):
                    k.dma("sp", qT[:, :], dq[0][h], [("dq", 0, h)], ["dn_qT"])
                    k.dma("sp", kT[:, :], dq[1][h], [("dq", 1, h)], ["dn_kT"])
                    k.dma("sp", vT[:, :], dq[2][h], [("dq", 2, h)], ["dn_vT"])
                    for (srcT, dstm, sk, dk_) in ((kT, ktm, "dn_kT", "dn_ktm"), (vT, vtm, "dn_vT", "dn_vtm")):
                        for c0 in range(0, NCK, 8):
                            ncb = min(8, NCK - c0)
                            ptr = self.psbf[pi % 2]
                            ktr = ("ps", 6 + pi % 2)
                            for c in range(ncb):
                                for d_ in range(2):
                                    k.tr(ptr[HS[d_], c * 128:(c + 1) * 128], srcT[:, (c0 + c) * 64:(c0 + c + 1) * 64], self.identb[:, :],
                                         [sk, "identb"], [ktr])
                            e = "act" if pi % 2 == 0 else "dve"
                            k.copy(e, dstm[:, c0:c0 + ncb, :], ptr[:, 0:ncb * 128].rearrange("p (c d) -> p c d", d=128), [ktr], [dk_])
                            pi += 1
                    k.act(coef[:, :], Gtm2[:, :, h], AF.Exp, [("Gtm2", 0), ("Gtm2", 1)], ["dn_coef"])
                    k.tt("dve", coef[:, :], coef[:, :], Btm2[:, :, h], ALU.mult, ["dn_coef", ("Btm2", 0), ("Btm2", 1)], ["dn_coef"])
                    GK = [("Gtm2", 0), ("Gtm2", 1)]
                    BK = [("Btm2", 0), ("Btm2", 1)]

                    def early(t0, n, bp):
                        Pb, Qb, Xf = PbS[bp], QbS[bp], XfS[bp]
                        PK = lambda i_: ("dn_P", bp, i_)
                        QKY = lambda i_: ("dn_Q", bp, i_)
                        XK = ("dn_Xf", bp)
                        nch = n // 64
                        c0 = t0 // 64
                        pG = self.psb[0]
                        for d_ in range(2):
                            k.mm(pG[HS[d_], :n], sel[0:8, h, 0:64], Gfm[d_][0:8, t0:t0 + n], True, True, ["dn_sel", ("Gfm", d_)], [("ps", 0)])
                        pG3 = pG[:, :n].rearrange("p (c f) -> p c f", f=64)
                        k.tt("dve", E[:, :n].rearrange("p (c f) -> p c f", f=64), pG3,
                             Gtm2[:, c0:c0 + nch, h:h + 1].to_broadcast([128, nch, 64]), ALU.subtract,
                             [("ps", 0)] + GK, ["dn_E"])
                        for d_ in range(2):
                            k.copy("act", glast[HS[d_], c0:c0 + nch], pG[HS[d_], :n].rearrange("p (c f) -> p c f", f=64)[:, :, LASTF[d_]],
                                   [("ps", 0)], ["dn_glast"])
                        yield
                        for d_ in range(2):
                            pG2 = self.psb[1 + d_]
                            kG2 = ("ps", 1 + d_)
                            k.mm(pG2[:, :n], sel[0:8, h, :], Gfm[d_][0:8, t0:t0 + n], True, True, ["dn_sel", ("Gfm", d_)], [kG2])
                            k.act(eG[:, :n], pG2[:, :n], AF.Exp, [kG2], ["dn_eG"])
                            k.act(DS[d_][:, c0:c0 + nch], pG2[:, :n].rearrange("p (c f) -> p c f", f=64)[:, :, LASTF[d_]], AF.Exp,
                                  [kG2], [("dn_DS", d_)])
                            k.tt("dve", QD[d_][:, t0:t0 + n], qT[:, t0:t0 + n], eG[:, :n], ALU.mult, ["dn_qT", "dn_eG"], [("dn_QD", d_)])
                        for d_ in range(2):
                            pB = self.psb[6 + d_]
                            kB = ("ps", 6 + d_)
                            k.mm(pB[:, :n], sel[0:8, h, :], Bfm[d_][0:8, t0:t0 + n], True, True, ["dn_sel", ("Bfm", d_)], [kB])
                            k.tt("dve", kbT[d_][:, :n], kT[:, t0:t0 + n], pB[:, :n], ALU.mult, ["dn_kT", kB], [("dn_kbT", d_)])
                        yield
                        E3 = E[:, :n].rearrange("p (c f) -> p c f", f=64)

                        def mk(idx):
                            return masks[:, idx:idx + 1, :].to_broadcast([128, nch, 64])
                        k.stt("dve", F1[:, :n].rearrange("p (c f) -> p c f", f=64), E3, 0.0, mk(0), ALU.min, ALU.add,
                              ["dn_E", "dn_masks"], ["dn_F1"])
                        k.act(gT_i[:, :n], F1[:, :n], AF.Exp, ["dn_F1"], ["dn_gTi"])
                        yield
                        k.stt("dve", F1[:, :n].rearrange("p (c f) -> p c f", f=64), E3, 0.0, mk(1), ALU.min, ALU.add,
                              ["dn_E", "dn_masks", "dn_F1"], ["dn_F1"])
                        k.act(gT_s[:, :n], F1[:, :n], AF.Exp, ["dn_F1"], ["dn_gTs"])
                        k.stt("dve", F1[:, :n].rearrange("p (c f) -> p c f", f=64), E3, 0.0, mk(2), ALU.max, ALU.add,
                              ["dn_E", "dn_masks", "dn_F1"], ["dn_F1"])
                        k.act(g_s[:, :n], F1[:, :n], AF.Exp, ["dn_F1"], ["dn_gs"], scale=-1.0)
                        pN, pNT, pKQ = self.psb[0], self.psb[1], self.psb[2]
                        for c in range(nch):
                            cs = slice(t0 + c * 64, t0 + (c + 1) * 64)
                            bs = slice(c * 64, (c + 1) * 64)
                            for d_ in range(2):
                                k.mm(pN[HS[d_], bs], kT[:, cs], kbT[d_][:, bs], True, True, ["dn_kT", ("dn_kbT", d_)], [("ps", 0)])
                                k.mm(pNT[HS[d_], bs], kbT[d_][:, bs], kT[:, cs], True, True, ["dn_kT", ("dn_kbT", d_)], [("ps", 1)])
                                k.mm(pKQ[HS[d_], bs], kT[:, cs], qT[:, cs], True, True, ["dn_kT", "dn_qT"], [("ps", 2)])
                        yield
                        k.stt("dve", Xf[:, :n], pN[:, :n], -1.0, gT_s[:, :n], ALU.mult, ALU.mult, [("ps", 0), "dn_gTs"], [XK])
                        k.tt("dve", Qb[0][:, :n], pNT[:, :n], g_s[:, :n], ALU.mult, [("ps", 1), "dn_gs"], [QKY(0)])
                        k.tt("dve", QK[:, t0:t0 + n], pKQ[:, :n], gT_i[:, :n], ALU.mult, [("ps", 2), "dn_gTi"], ["dn_QK"])
                        k.act(Pb[0][:, :n], Xf[:, :n], AF.Copy, [XK], [PK(0)], scale=-1.0)
                        k.tt("pool", Xf[:, :n].rearrange("p (c f) -> p c f", f=64), Xf[:, :n].rearrange("p (c f) -> p c f", f=64),
                             ident2[:, :].unsqueeze(1).to_broadcast([128, nch, 64]), ALU.add, [XK, "dn_ident2"], [XK])
                        yield

                    def doubling(t0, n, bp):
                        Pb, Qb, Xf, Xb = PbS[bp], QbS[bp], XfS[bp], XbS[bp]
                        PK = lambda i_: ("dn_P", bp, i_)
                        QKY = lambda i_: ("dn_Q", bp, i_)
                        XK = ("dn_Xf", bp)
                        nch = n // 64
                        pP, pQ, pX = self.psb[3], self.psb[4], self.psb[5]
                        for st_ in range(1, 6):
                            a_, b_ = (st_ - 1) % 2, st_ % 2
                            if st_ < 5:
                                for c in range(nch):
                                    bs = slice(c * 64, (c + 1) * 64)
                                    for d_ in range(2):
                                        k.mm(pP[HS[d_], bs], Qb[a_][HS[d_], bs], Pb[a_][HS[d_], bs], True, True,
                                             [PK(a_), QKY(a_)], [("ps", 3)])
                            for c in range(nch):
                                bs = slice(c * 64, (c + 1) * 64)
                                for d_ in range(2):
                                    k.mm(pQ[HS[d_], bs], Pb[a_][HS[d_], bs], Qb[a_][HS[d_], bs], True, True,
                                         [PK(a_), QKY(a_)], [("ps", 4)])
                            yield
                            if st_ < 5:
                                k.copy("act", Pb[b_][:, :n], pP[:, :n], [("ps", 3)], [PK(b_)])
                            k.copy("dve", Qb[b_][:, :n], pQ[:, :n], [("ps", 4)], [QKY(b_)])
                            yield
                            for c in range(nch):
                                bs = slice(c * 64, (c + 1) * 64)
                                for d_ in range(2):
                                    k.mm(pX[HS[d_], bs], Qb[b_][HS[d_], bs], Xf[HS[d_], bs], True, True,
                                         [QKY(b_), XK], [("ps", 5)])
                            yield
                            k.tt("dve", Xf[:, :n], Xf[:, :n], pX[:, :n], ALU.add, [XK, ("ps", 5)], [XK])
                            yield
                        k.copy("act", Xb[:, :n], Xf[:, :n], [XK], [("dn_Xb", bp)])

                    def late(t0, n, bp):
                        Xb = XbS[bp]
                        XBK = ("dn_Xb", bp)
                        nch = n // 64
                        c0 = t0 // 64
                        k.tt("pool", vb[:, :nch, :], vtm[:, c0:c0 + nch, :], Btm2[:, c0:c0 + nch, h:h + 1].to_broadcast([128, nch, 128]),
                             ALU.mult, ["dn_vtm"] + BK, ["dn_vb"])
                        k.tt("pool", kg[:, :nch, :], ktm[:, c0:c0 + nch, :], coef[:, c0:c0 + nch].unsqueeze(2).to_broadcast([128, nch, 128]),
                             ALU.mult, ["dn_ktm", "dn_coef"], ["dn_kg"])
                        k.tt("dve", dtm[:, c0:c0 + nch], glast[:, c0:c0 + nch], Gtm2[:, c0:c0 + nch, h], ALU.subtract,
                             ["dn_glast"] + GK, ["dn_dtm"])
                        k.act(dtm[:, c0:c0 + nch], dtm[:, c0:c0 + nch], AF.Exp, ["dn_dtm"], ["dn_dtm"])
                        k.tt("pool", KD[:, c0:c0 + nch, :], ktm[:, c0:c0 + nch, :],
                             dtm[:, c0:c0 + nch].unsqueeze(2).to_broadcast([128, nch, 128]), ALU.mult, ["dn_ktm", "dn_dtm"], ["dn_KD"])
                        for half in range(0, nch, 4):
                            pU = self.psb[6 + half // 4 % 2]
                            kU = ("ps", 6 + half // 4 % 2)
                            nh = min(4, nch - half)
                            for c in range(nh):
                                cc = half + c
                                for d_ in range(2):
                                    k.mm(pU[HS[d_], c * 128:(c + 1) * 128], Xb[HS[d_], cc * 64:(cc + 1) * 64], vb[HS[d_], cc, :], True, True,
                                         [XBK, "dn_vb"], [kU])
                            k.copy("act", U[:, c0 + half:c0 + half + nh, :], pU[:, 0:nh * 128].rearrange("p (c d) -> p c d", d=128),
                                   [kU], ["dn_U"])
                        for d_ in range(2):
                            pW = self.psb[1 + d_]
                            kW = ("ps", 1 + d_)
                            for c in range(nch):
                                k.mm(pW[:, c * 64:(c + 1) * 64], kg[HS[d_], c, :], Xb[HS[d_], c * 64:(c + 1) * 64], True, True, ["dn_kg", XBK], [kW])
                            k.copy("act" if d_ == 0 else "dve", WT[d_][:, t0:t0 + n], pW[:, :n], [kW], [("dn_WT", d_)])

                    def run(g):
                        for _ in g:
                            pass
                    run(early(TB[0][0], TB[0][1], 0))
                    for bi_ in range(len(TB)):
                        t0, n = TB[bi_]
                        dg_ = doubling(t0, n, bi_ % 2)
                        if bi_ + 1 < len(TB):
                            eg_ = early(TB[bi_ + 1][0], TB[bi_ + 1][1], (bi_ + 1) % 2)
                            live = [dg_, eg_]
                            while live:
                                for g_ in list(live):
                                    if next(g_, "done") == "done":
                                        live.remove(g_)
                        else:
                            run(dg_)
                        late(t0, n, bi_ % 2)
                    if getattr(self, "stop_after", "") == "dnpre":
                        self.tap("tap_U", U[:, :, :], [128, NCK, 128], "dn_U", BF16)
                        self.tap("tap_WT0", WT[0][:, :], [128, T], ("dn_WT", 0), BF16)
                        self.tap("tap_WT1", WT[1][:, :], [128, T], ("dn_WT", 1), BF16)
                        self.tap("tap_Xf", XfS[0][:, :], [128, 512], ("dn_Xf", 0))
                        return
                    for d_ in range(2):
                        k.memset("pool", S[d_][:, :], 0.0, [("dn_S", d_)])
                        k.memset("pool", Sb[d_][:, :], 0.0, [("dn_Sb", d_)])
                    order = [list(range(32, 36)) + list(range(0, 32)), list(range(35, 31, -1)) + list(range(31, -1, -1))]
                    for step in range(NCK):
                        for d_ in range(2):
                            c = order[d_][step]
                            cs = slice(c * 64, (c + 1) * 64)
                            hs = HS[d_]
                            p1, p2, p3 = self.psb[d_ * 3], self.psb[d_ * 3 + 1], self.psb[d_ * 3 + 2]
                            k1, k2, k3 = ("ps", d_ * 3), ("ps", d_ * 3 + 1), ("ps", d_ * 3 + 2)
                            vk = ("dn_vn", d_)
                            k.mm(p1[hs, 0:128], WT[d_][:, cs], Sb[d_][:, :], True, True, [("dn_WT", d_), ("dn_Sb", d_)], [k1])
                            k.tt("dve", vn[hs, :], U[hs, c, :], p1[hs, 0:128], ALU.subtract, ["dn_U", k1], [vk])
                            k.mm(p2[:, 0:64], Sb[d_][:, :], QD[d_][:, cs], True, False, [("dn_Sb", d_), ("dn_QD", d_)], [k2])
                            k.mm(p2[:, 0:64], vn[hs, :], QK[hs, cs], False, True, [vk, "dn_QK"], [k2])
                            k.copy("act", oacc[d_][:, cs], p2[:, 0:64], [k2], [("dn_oacc", d_)])
                            k.mm(p3[:, 0:128], KD[hs, c, :], vn[hs, :], True, True, ["dn_KD", vk], [k3])
                            k.stt("dve", Sb[d_][:, :], S[d_][:, :], DS[d_][:, c:c + 1], p3[:, 0:128], ALU.mult, ALU.add,
                                  [("dn_S", d_), ("dn_DS", d_), k3], [("dn_Sb", d_)])
                            k.stt("dve", S[d_][:, :], S[d_][:, :], DS[d_][:, c:c + 1], p3[:, 0:128], ALU.mult, ALU.add,
                                  [("dn_S", d_), ("dn_DS", d_), k3], [("dn_S", d_)])
                    k.dma("sp", zs[:, :], dq[3][h], [("dq", 3, h)], ["dn_qT"])
                    for (t0, n) in TB:
                        k.tt("dve", osum[:, :n], oacc[0][:, t0:t0 + n], oacc[1][:, t0:t0 + n], ALU.add, [("dn_oacc", 0), ("dn_oacc", 1)], ["dn_eG"])
                        k.tt("pool", osq[:, :n], osum[:, :n], osum[:, :n], ALU.mult, ["dn_eG"], [("dn_kbT", 0)])
                        pn = self.psb[6]
                        k.mm(pn[:, :n], self.onesb[:, :], osq[:, :n], True, True, ["onesb", ("dn_kbT", 0)], [("ps", 6)])
                        k.ts("dve", orn[:, :n], pn[:, :n], 1.0 / 128.0, RMS_EPS, ALU.mult, ALU.add, [("ps", 6)], ["dn_orn"])
                        k.act(orn[:, :n], orn[:, :n], AF.Sqrt, ["dn_orn"], ["dn_orn"])
                        k.recip(orn[:, :n], orn[:, :n], ["dn_orn"], ["dn_orn"])
                        k.stt("dve", osum[:, :n], osum[:, :n], self.vecs[:, ngo:ngo + 1], orn[:, :n], ALU.mult, ALU.mult,
                              ["dn_eG", "vecs", "dn_orn"], ["dn_eG"])
                        k.tt("pool", ogt[:, t0:t0 + n], osum[:, :n], zs[:, t0:t0 + n], ALU.mult, ["dn_eG", "dn_qT"], ["dn_vT"])
                    k.dma("sp", dn_og[h], ogt[:, :], ["dn_vT"], ["dn_og"], append=(h > 0))
                    if getattr(self, "stop_after", "") == "dnhead0":
                        self.tap("tap_ogt", ogt[:, :], [128, T], "dn_vT", BF16)
                        return
        k.fence()
        with ExitStack() as es:
            self.mixer_out(es, layer, self.dn_w_out, None, src, dst, 9, "xs1", None, og_dram=dn_og)


def prep_inputs(inp, b):
    consts = make_consts()
    xin = np.ascontiguousarray(np.concatenate([inp["x"][b].T, inp["ctx"][b].T], axis=1))
    cv = np.stack([inp["c"][b].reshape(8, 128).T, inp["c_ctx"].reshape(8, 128).T], axis=2).reshape(128, 16)
    m = {
        "xin": xin.astype(np.float32),
        "cvec": np.ascontiguousarray(cv, dtype=np.float32),
        "vecs": pack_vecs(inp),
    }
    for nm in CONST_SHAPES:
        m[nm] = consts[nm]
    for nm in ("ada_w", "ffn_w_gate", "ffn_w_up", "ffn_w_down"):
        m[nm] = np.ascontiguousarray(inp[nm], dtype=np.float32)
    for nm in ("da_w_in", "da_w_out", "dn_w_in", "dn_w_out"):
        m[nm] = np.ascontiguousarray(inp[nm][0], dtype=np.float32)
    return m


def kernel(**inputs):
    inp = {k_: np.asarray(v) for k_, v in inputs.items()}
    p = Prog([("ada", 0), ("dn", 0), ("ffn", 0), ("ada", 1), ("da", 1), ("ffn", 1)])
    nc = p.build()
    in_maps = [prep_inputs(inp, b) for b in range(8)]
    res = run_bass_kernel_spmd(nc, in_maps, core_ids=list(range(8)))
    out = np.stack([np.ascontiguousarray(r["outT"].T) for r in res.results], axis=0)
    return out.astype(np.float32)
```
